# Optimizing a Trainium2 kernel written in Bass

```python
import math
import jax, jax.numpy as jnp
from jax import lax
import numpy as np

D_MODEL = 1024
BATCH = 8
SEQ = 4096
DEPTH = 4

CONV_DIM = 512
CONV_WIDTH = 31
MLA_HEADS = 8
MLA_NOPE = 64
MLA_ROPE = 32
MLA_V = 64
MLA_Q_RANK = 256
MLA_KV_RANK = 128
ROPE_THETA = 10000.0
DIFF_HEADS = 4
DIFF_HEAD = 64
DIFF_V = 2 * DIFF_HEAD
REL_BUCKETS = 32
REL_MAX_DIST = 128
D_FF = 2816
FFN_CONV = 3
N_BRANCH = 3
Q_BLOCK = 128
EPS = 1e-6

W_GLU = 2 * CONV_DIM
W_DQ = DIFF_HEADS * 2 * DIFF_HEAD
W_DK = DIFF_HEADS * 2 * DIFF_HEAD
W_DV = DIFF_HEADS * DIFF_V
W_GATE = N_BRANCH * D_MODEL
D_IN = W_GLU + MLA_Q_RANK + MLA_KV_RANK + MLA_ROPE + W_DQ + W_DK + W_DV + W_GATE

kernel_name = 'hybrid_conv_mla_diffattn_gated_trunk'


def rmsnorm(x, g):
    x32 = x.astype(jnp.float32)
    y = x32 * lax.rsqrt(jnp.mean(x32 * x32, axis=-1, keepdims=True) + EPS)
    return (y * g.astype(jnp.float32)).astype(x.dtype)


def layernorm(x, g, b):
    x32 = x.astype(jnp.float32)
    mu = jnp.mean(x32, axis=-1, keepdims=True)
    var = jnp.mean(jnp.square(x32 - mu), axis=-1, keepdims=True)
    y = (x32 - mu) * lax.rsqrt(var + EPS)
    return (y * g.astype(jnp.float32) + b.astype(jnp.float32)).astype(x.dtype)


def causal_dwconv(x, w, b):
    k = w.shape[0]
    y = lax.conv_general_dilated(
        x, w[:, None, :].astype(x.dtype), window_strides=(1,), padding=[(k - 1, 0)],
        dimension_numbers=('NWC', 'WIO', 'NWC'), feature_group_count=x.shape[-1])
    return y + b.astype(x.dtype)


def rope(x, positions):
    half = x.shape[-1] // 2
    freqs = ROPE_THETA ** (-jnp.arange(half, dtype=jnp.float32) / half)
    ang = positions.astype(jnp.float32)[..., None] * freqs
    ang = ang.reshape(ang.shape[:2] + (1,) * (x.ndim - 3) + (half,))
    cos = jnp.cos(ang).astype(x.dtype)
    sin = jnp.sin(ang).astype(x.dtype)
    x1, x2 = x[..., :half], x[..., half:]
    return jnp.concatenate([x1 * cos - x2 * sin, x2 * cos + x1 * sin], axis=-1)


def rel_bucket(n):
    n = jnp.maximum(n, 0)
    max_exact = REL_BUCKETS // 2
    nf = jnp.maximum(n, 1).astype(jnp.float32)
    large = max_exact + (jnp.log(nf / max_exact) / math.log(REL_MAX_DIST / max_exact)
                         * (REL_BUCKETS - max_exact)).astype(jnp.int32)
    large = jnp.minimum(large, REL_BUCKETS - 1)
    return jnp.where(n < max_exact, n, large)


def to_blocks(t):
    b, s = t.shape[:2]
    t = t.reshape((b, s // Q_BLOCK, Q_BLOCK) + t.shape[2:])
    return jnp.moveaxis(t, 1, 0)


def from_blocks(t):
    t = jnp.moveaxis(t, 0, 1)
    return t.reshape((t.shape[0], t.shape[1] * t.shape[2]) + t.shape[3:])


def mla_attention(c_q, c_kv, k_rope, positions, g_q, w_uq, g_kv, w_ukv):
    b, s, _ = c_q.shape
    q = (rmsnorm(c_q, g_q) @ w_uq).reshape(b, s, MLA_HEADS, MLA_NOPE + MLA_ROPE)
    q_nope, q_pe = q[..., :MLA_NOPE], rope(q[..., MLA_NOPE:], positions)
    kv = (rmsnorm(c_kv, g_kv) @ w_ukv).reshape(b, s, MLA_HEADS, MLA_NOPE + MLA_V)
    k_nope, v = kv[..., :MLA_NOPE], kv[..., MLA_NOPE:]
    k_pe = rope(k_rope, positions)
    scale = (MLA_NOPE + MLA_ROPE) ** -0.5
    kidx = jnp.arange(s)
    qidx = kidx.reshape(s // Q_BLOCK, Q_BLOCK)

    def block(args):
        qn_b, qp_b, qi = args
        sc = (jnp.einsum('bqhd,bkhd->bhqk', qn_b, k_nope)
              + jnp.einsum('bqhr,bkr->bhqk', qp_b, k_pe)).astype(jnp.float32) * scale
        sc = jnp.where(kidx[None, :] <= qi[:, None], sc, -jnp.inf)
        p = jax.nn.softmax(sc, axis=-1).astype(v.dtype)
        return jnp.einsum('bhqk,bkhd->bqhd', p, v)

    o = lax.map(block, (to_blocks(q_nope), to_blocks(q_pe), qidx))
    return from_blocks(o).reshape(b, s, MLA_HEADS * MLA_V)


def diff_attention(q, k, v, positions, rel_table, lq1, lk1, lq2, lk2, g_sub, lam_init):
    b, s, _ = q.shape
    q = q.reshape(b, s, DIFF_HEADS, 2, DIFF_HEAD)
    k = k.reshape(b, s, DIFF_HEADS, 2, DIFF_HEAD)
    v = v.reshape(b, s, DIFF_HEADS, DIFF_V)
    f32 = jnp.float32
    lam = (jnp.exp(jnp.sum(lq1.astype(f32) * lk1.astype(f32)))
           - jnp.exp(jnp.sum(lq2.astype(f32) * lk2.astype(f32))) + lam_init)
    scale = DIFF_HEAD ** -0.5
    kidx = jnp.arange(s)
    qidx = kidx.reshape(s // Q_BLOCK, Q_BLOCK)

    def block(args):
        q_b, pos_b, qi = args
        sc = jnp.einsum('bqhmd,bkhmd->mbhqk', q_b, k).astype(f32) * scale
        bucket = rel_bucket(pos_b[:, :, None] - positions[:, None, :])
        bias = jnp.transpose(rel_table[bucket], (0, 3, 1, 2)).astype(f32)
        sc = jnp.where(kidx[None, :] <= qi[:, None], sc + bias[None], -jnp.inf)
        p = jax.nn.softmax(sc, axis=-1)
        a = (p[0] - lam * p[1]).astype(v.dtype)
        return jnp.einsum('bhqk,bkhd->bqhd', a, v)

    o = from_blocks(lax.map(block, (to_blocks(q), to_blocks(positions), qidx)))
    o = rmsnorm(o, g_sub) * (1.0 - lam_init)
    return o.reshape(b, s, DIFF_HEADS * DIFF_V)


def setup_inputs(seed: int = 0) -> dict:
    key = jax.random.key(seed)
    ks = jax.random.split(key, 40)
    f32 = jnp.float32

    def nrm(k, shape, scale):
        return jax.random.normal(k, shape, f32) * scale

    def gain(k, shape):
        return 1.0 + 0.02 * jax.random.normal(k, shape, f32)

    L = DEPTH
    x = nrm(ks[0], (BATCH, SEQ, D_MODEL), 1.0)
    offs = jax.random.randint(ks[1], (BATCH, 1), 0, 1024, dtype=jnp.int32)
    positions = offs + jnp.arange(SEQ, dtype=jnp.int32)[None, :]
    return {
        'x': x,
        'positions': positions,
        'rel_bias': nrm(ks[2], (REL_BUCKETS, DIFF_HEADS), 0.5),
        'norm_mix': gain(ks[3], (L, D_MODEL)),
        'w_in': nrm(ks[4], (L, D_MODEL, D_IN), D_MODEL ** -0.5),
        'gate_bias': nrm(ks[5], (L, W_GATE), 0.02),
        'conv_w': nrm(ks[6], (L, CONV_WIDTH, CONV_DIM), CONV_WIDTH ** -0.5),
        'conv_b': nrm(ks[7], (L, CONV_DIM), 0.02),
        'conv_ln_g': gain(ks[8], (L, CONV_DIM)),
        'conv_ln_b': nrm(ks[9], (L, CONV_DIM), 0.02),
        'w_conv_out': nrm(ks[10], (L, CONV_DIM, D_MODEL), CONV_DIM ** -0.5),
        'mla_q_norm': gain(ks[11], (L, MLA_Q_RANK)),
        'w_uq': nrm(ks[12], (L, MLA_Q_RANK, MLA_HEADS * (MLA_NOPE + MLA_ROPE)), MLA_Q_RANK ** -0.5),
        'mla_kv_norm': gain(ks[13], (L, MLA_KV_RANK)),
        'w_ukv': nrm(ks[14], (L, MLA_KV_RANK, MLA_HEADS * (MLA_NOPE + MLA_V)), MLA_KV_RANK ** -0.5),
        'w_mla_out': nrm(ks[15], (L, MLA_HEADS * MLA_V, D_MODEL), (MLA_HEADS * MLA_V) ** -0.5),
        'diff_lam_q1': nrm(ks[16], (L, DIFF_HEAD), 0.1),
        'diff_lam_k1': nrm(ks[17], (L, DIFF_HEAD), 0.1),
        'diff_lam_q2': nrm(ks[18], (L, DIFF_HEAD), 0.1),
        'diff_lam_k2': nrm(ks[19], (L, DIFF_HEAD), 0.1),
        'diff_sub_norm': gain(ks[20], (L, DIFF_V)),
        'w_diff_out': nrm(ks[21], (L, DIFF_HEADS * DIFF_V, D_MODEL), (DIFF_HEADS * DIFF_V) ** -0.5),
        'w_out': nrm(ks[22], (L, D_MODEL, D_MODEL), D_MODEL ** -0.5),
        'norm_ffn': gain(ks[23], (L, D_MODEL)),
        'w_up': nrm(ks[24], (L, D_MODEL, 2 * D_FF), D_MODEL ** -0.5),
        'ffn_conv_w': nrm(ks[25], (L, FFN_CONV, 2 * D_FF), FFN_CONV ** -0.5),
        'ffn_conv_b': nrm(ks[26], (L, 2 * D_FF), 0.02),
        'w_down': nrm(ks[27], (L, D_FF, D_MODEL), D_FF ** -0.5),
        'norm_final': gain(ks[28], (D_MODEL,)),
    }


def reference(x, positions, rel_bias, norm_mix, w_in, gate_bias, conv_w, conv_b, conv_ln_g,
              conv_ln_b, w_conv_out, mla_q_norm, w_uq, mla_kv_norm, w_ukv, w_mla_out,
              diff_lam_q1, diff_lam_k1, diff_lam_q2, diff_lam_k2, diff_sub_norm, w_diff_out,
              w_out, norm_ffn, w_up, ffn_conv_w, ffn_conv_b, w_down, norm_final):
    b, s, d = x.shape
    cuts = list(np.cumsum([W_GLU, MLA_Q_RANK, MLA_KV_RANK, MLA_ROPE, W_DQ, W_DK, W_DV]))
    for l in range(DEPTH):
        h = rmsnorm(x, norm_mix[l])
        z = h @ w_in[l]
        u_glu, c_q, c_kv, k_rope, dq, dk, dv, gates = jnp.split(z, cuts, axis=-1)

        a = u_glu[..., :CONV_DIM] * jax.nn.sigmoid(u_glu[..., CONV_DIM:])
        a = causal_dwconv(a, conv_w[l], conv_b[l])
        a = jax.nn.silu(layernorm(a, conv_ln_g[l], conv_ln_b[l]))
        y_a = a @ w_conv_out[l]

        y_b = mla_attention(c_q, c_kv, k_rope, positions, mla_q_norm[l], w_uq[l],
                            mla_kv_norm[l], w_ukv[l]) @ w_mla_out[l]

        lam_init = 0.8 - 0.6 * math.exp(-0.3 * l)
        y_c = diff_attention(dq, dk, dv, positions, rel_bias, diff_lam_q1[l], diff_lam_k1[l],
                             diff_lam_q2[l], diff_lam_k2[l], diff_sub_norm[l], lam_init) @ w_diff_out[l]

        g = jax.nn.sigmoid(gates + gate_bias[l]).reshape(b, s, N_BRANCH, d)
        merged = g[:, :, 0] * y_a + g[:, :, 1] * y_b + g[:, :, 2] * y_c
        x = x + merged @ w_out[l]

        h = rmsnorm(x, norm_ffn[l])
        u = causal_dwconv(h @ w_up[l], ffn_conv_w[l], ffn_conv_b[l])
        x = x + (jax.nn.silu(u[..., :D_FF]) * u[..., D_FF:]) @ w_down[l]
    return rmsnorm(x, norm_final)
```

```python
import contextlib
import math

import numpy as np
import concourse.bass as bass
import concourse.mybir as mybir
from concourse.bass_utils import run_bass_kernel_spmd

F32 = mybir.dt.float32
BF16 = mybir.dt.bfloat16
I32 = mybir.dt.int32
ALU = mybir.AluOpType
AF = mybir.ActivationFunctionType
AX = mybir.AxisListType

D = 1024
S = 4096
L = 4
NTG = 8
TG = 512
EPS = 1e-6
D_FF = 2816
NBLK = 32
BLK = 4096

EPOCH = 8000
NDMASEM = 16


class Op:
    __slots__ = ("eng", "fn", "deps", "needs_inc", "dma", "cnt", "dsem", "dval")

    def __init__(self, eng, fn, dma):
        self.eng = eng
        self.fn = fn
        self.deps = []
        self.needs_inc = False
        self.dma = dma
        self.cnt = 0
        self.dsem = None
        self.dval = 0


class Prog:
    ENGS = ("pe", "act", "dve", "pool", "sp")
    BK = 2048

    def __init__(self, nc):
        self.nc = nc
        self.ops = {e: [] for e in self.ENGS}
        self.live = {}
        self.meta = {}
        self.nops = 0
        self._uid = 0

    def reg(self, t, rowsz, space=None, base=0, esz=1):
        self.meta[t.name] = (rowsz, space if space else t.name, base, esz)
        return t

    def sbuf(self, st, name, shape, dt):
        nc = self.nc
        addr = (nc.sbuf_base + 31) // 32 * 32
        self._uid += 1
        t = st.enter_context(nc.sbuf_tensor(f"sb{self._uid}_{name}", shape, dt))
        esz = 2 if dt == BF16 else 4
        return self.reg(t, int(np.prod(shape[1:])), "sb", addr, esz)

    def psum(self, st, name, shape, dt):
        nc = self.nc
        addr = nc.psum_base * 2048
        t = st.enter_context(nc.psum_tensor(name, shape, dt))
        return self.reg(t, int(np.prod(shape[1:])), "ps", addr, 4)

    def dram(self, name, shape, dt, kind):
        t = self.nc.dram_tensor(name, shape, dt, kind=kind)
        return self.reg(t, int(shape[-1]))

    def region(self, ap):
        C, space, base, esz = self.meta[ap.tensor.name]
        off = int(ap.offset)
        r0, c0 = divmod(off, C)
        re, ce = 0, 0
        for step, cnt in ap.ap:
            if cnt <= 1 or step == 0:
                continue
            if step % C == 0:
                re += (step // C) * (cnt - 1)
            else:
                ce += step * (cnt - 1)
        return (space, r0, r0 + re + 1, base + c0 * esz, base + (c0 + ce + 1) * esz)

    def _buckets(self, rg):
        if rg[0] in ("sb", "ps"):
            return [(rg[0], k) for k in range(rg[3] // self.BK, (rg[4] - 1) // self.BK + 1)]
        return [(rg[0], 0)]

    def op(self, eng, fn, reads=(), writes=(), dma=False):
        o = Op(eng, fn, dma)
        self.nops += 1
        deps = {}
        rregs = [self.region(a) for a in reads]
        wregs = [self.region(a) for a in writes]
        live = self.live
        for rg in rregs:
            for bk in self._buckets(rg):
                d = live.get(bk)
                if not d:
                    continue
                for ent in d.values():
                    if ent[5] and ent[1] < rg[2] and rg[1] < ent[2] and ent[3] < rg[4] and rg[3] < ent[4]:
                        deps[id(ent[6])] = ent[6]
        for rg in wregs:
            for bk in self._buckets(rg):
                d = live.get(bk)
                if not d:
                    continue
                dead = []
                for key, ent in d.items():
                    if ent[1] < rg[2] and rg[1] < ent[2] and ent[3] < rg[4] and rg[3] < ent[4]:
                        deps[id(ent[6])] = ent[6]
                        if rg[1] <= ent[1] and ent[2] <= rg[2] and rg[3] <= ent[3] and ent[4] <= rg[4]:
                            dead.append(key)
                for key in dead:
                    del d[key]
        for d in deps.values():
            if d is o:
                continue
            if d.eng == "pe" and eng == "pe" and not d.dma and not dma:
                continue
            d.needs_inc = True
            o.deps.append(d)
        for rg in rregs:
            if dma:
                self._uid += 1
                key = ("r", self._uid)
            else:
                key = ("r", eng, rg[1], rg[2], rg[3], rg[4])
            ent = (rg[0], rg[1], rg[2], rg[3], rg[4], False, o)
            for bk in self._buckets(rg):
                live.setdefault(bk, {})[key] = ent
        for rg in wregs:
            self._uid += 1
            key = ("w", self._uid)
            ent = (rg[0], rg[1], rg[2], rg[3], rg[4], True, o)
            for bk in self._buckets(rg):
                live.setdefault(bk, {})[key] = ent
        self.ops[eng].append(o)
        return o

    def emit(self, final_ops=()):
        nc = self.nc
        nsem_eng = {}
        for e in self.ENGS:
            c = 0
            j = 0
            for o in self.ops[e]:
                if o.dma:
                    o.dsem = (e, j % NDMASEM)
                    o.dval = 16 * (j // NDMASEM + 1)
                    j += 1
                elif o.needs_inc:
                    c += 1
                    o.cnt = c
            nsem_eng[e] = (c + EPOCH - 1) // EPOCH
        sems = {}
        stack = contextlib.ExitStack()
        for e in self.ENGS:
            for k in range(max(nsem_eng[e], 1)):
                sems[(e, "c", k)] = stack.enter_context(nc.semaphore(f"s_{e}_c{k}"))
            if any(o.dma for o in self.ops[e]):
                for k in range(NDMASEM):
                    sems[(e, "d", k)] = stack.enter_context(nc.semaphore(f"s_{e}_d{k}"))
        prog = self

        def sig(o):
            if o.dma:
                return (o.dsem[0], "d", o.dsem[1]), o.dval
            ep, loc = divmod(o.cnt - 1, EPOCH)
            return (o.eng, "c", ep), loc + 1

        def run_engine(e, eng):
            waited = {}

            def do_wait(key, val):
                if waited.get(key, 0) >= val:
                    return
                waited[key] = val
                eng.wait_ge(sems[key], val)

            for o in prog.ops[e]:
                need = {}
                for d in o.deps:
                    key, val = sig(d)
                    if need.get(key, 0) < val:
                        need[key] = val
                if o.dma and o.dval > 16:
                    key = (e, "d", o.dsem[1])
                    if need.get(key, 0) < o.dval - 16:
                        need[key] = o.dval - 16
                for key, val in need.items():
                    do_wait(key, val)
                ins = o.fn(eng)
                if o.dma:
                    ins.then_inc(sems[(e, "d", o.dsem[1])], 16)
                elif o.needs_inc:
                    ep, loc = divmod(o.cnt - 1, EPOCH)
                    ins.then_inc(sems[(e, "c", ep)], 1)
            if e == "sp":
                for o in final_ops:
                    key, val = sig(o)
                    do_wait(key, val)

        with stack:
            with nc.Block() as block:
                @block.tensor
                def _(eng):
                    run_engine("pe", eng)

                @block.scalar
                def _(eng):
                    run_engine("act", eng)

                @block.vector
                def _(eng):
                    run_engine("dve", eng)

                @block.gpsimd
                def _(eng):
                    run_engine("pool", eng)

                @block.sync
                def _(eng):
                    run_engine("sp", eng)


COLP_SPEC = [
    ("norm_mix", 8), ("gate_bias", 24), ("conv_b", 4), ("conv_ln_g", 4), ("conv_ln_b", 4),
    ("mla_q_norm", 2), ("mla_kv_norm", 1), ("diff_sub_norm", 1), ("norm_ffn", 8),
    ("ffn_conv_b", 44), ("conv_w", 31 * 4), ("ffn_conv_w", 3 * 44),
]
COLP_OFF = {}
_o = 0
for _n, _c in COLP_SPEC:
    COLP_OFF[_n] = (_o, _c)
    _o += L * _c
COLP_OFF["norm_final"] = (_o, 8)
_o += 8
COLP_OFF["freq"] = (_o, 1)
_o += 1
NCOLP = _o


def cpi(name, l, c):
    off, n = COLP_OFF[name]
    return off + l * n + c


def host_colp(inp):
    colp = np.zeros((128, NCOLP), np.float32)
    for name, n in COLP_SPEC:
        a = np.asarray(inp[name], np.float32).reshape(L, n, 128)
        off = COLP_OFF[name][0]
        colp[:, off:off + L * n] = a.reshape(L * n, 128).T
    off = COLP_OFF["norm_final"][0]
    colp[:, off:off + 8] = np.asarray(inp["norm_final"], np.float32).reshape(8, 128).T
    fr = np.float32(10000.0) ** (-np.arange(16, dtype=np.float32) / np.float32(16))
    colp[64:96, COLP_OFF["freq"][0]] = np.concatenate([fr, fr])
    return colp


def kmaj(w, kc):
    K, N = w.shape
    return np.ascontiguousarray(w.reshape(kc, 128, N).transpose(1, 0, 2))


def host_weights(inp):
    w_in = np.asarray(inp["w_in"], np.float32)
    w1 = np.stack([kmaj(w_in[l][:, 1024:2976], 8) for l in range(L)])
    wuq = np.stack([kmaj(np.asarray(inp["w_uq"][l], np.float32), 2) for l in range(L)])
    wukv = np.ascontiguousarray(np.asarray(inp["w_ukv"], np.float32))
    ws = np.zeros((L, NBLK, 128, BLK), np.float32)
    for l in range(L):
        b = 0
        wi = w_in[l]
        for j in range(2):
            cols = np.concatenate([np.arange(256 * j, 256 * j + 256), 512 + np.arange(256 * j, 256 * j + 256)])
            ws[l, b] = kmaj(wi[:, cols], 8).reshape(128, BLK)
            b += 1
        for br, nm in enumerate(("w_conv_out", "w_mla_out", "w_diff_out")):
            ws[l, b] = kmaj(np.asarray(inp[nm][l], np.float32), 4).reshape(128, BLK)
            b += 1
            for half in range(2):
                c0 = 2976 + br * 1024 + half * 512
                ws[l, b] = kmaj(wi[:, c0:c0 + 512], 8).reshape(128, BLK)
                b += 1
        wo = np.asarray(inp["w_out"][l], np.float32)
        for half in range(2):
            ws[l, b] = kmaj(wo[:, half * 512:(half + 1) * 512], 8).reshape(128, BLK)
            b += 1
        wu = np.asarray(inp["w_up"][l], np.float32)
        for k in range(11):
            cols = np.concatenate([np.arange(256 * k, 256 * k + 256), D_FF + np.arange(256 * k, 256 * k + 256)])
            ws[l, b] = kmaj(wu[:, cols], 8).reshape(128, BLK)
            b += 1
        wd = np.asarray(inp["w_down"][l], np.float32)
        for oc in range(8):
            ws[l, b, :, :22 * 128] = kmaj(wd[:, oc * 128:(oc + 1) * 128], 22).reshape(128, 22 * 128)
            b += 1
        assert b == NBLK
    return w1, wuq, wukv, ws


T5_THR = [int(math.ceil(16.0 * 8.0 ** (m / 16.0))) for m in range(1, 16)]


def build(layers, first, final, dbg=False):
    nc = bass.Bass("TRN2", target_bir_lowering=False)
    P = Prog(nc)
    nl = len(layers)

    def din(name, shape, dt):
        return P.dram(name, shape, dt, "ExternalInput").ap()

    def dscr(name, shape, dt):
        return P.dram(name, shape, dt, "ExternalOutput" if dbg else "Internal").ap()

    xin = din("xin", [D, S], F32)
    pos_d = din("pos", [1, S], I32)
    colp_d = din("colp", [128, NCOLP], F32)
    relb_d = din("relb", [1, 128], F32)
    lamv_d = din("lamv", [1, 4 * L * 64], F32)
    w1_d = din("w1", [nl, 128, 8 * 1952], F32)
    wuq_d = din("wuq", [nl, 128, 2 * 768], F32)
    wukv_d = din("wukv", [nl, 128, 1024], F32)
    ws_d = din("ws", [nl * NBLK, 128, BLK], F32)
    yout = P.dram("yout", [D, S], F32, "ExternalOutput").ap()

    xres = dscr("xres", [D, S], F32)
    rope_d = dscr("rope", [2 * 32, S], F32)
    QT_d = dscr("QT", [96, 8 * S], BF16)
    KT_d = dscr("KT", [96, 8 * S], BF16)
    DQ_d = dscr("DQ", [128, 4 * S], BF16)
    DK_d = dscr("DK", [128, 4 * S], BF16)
    oB_d = dscr("oB", [512, S], BF16)
    oC_d = dscr("oC", [512, S], BF16)
    wsb_d = P.dram("wsb", [NBLK, 128, BLK], BF16, "Internal").ap()

    def aps(*xs):
        return [x for x in xs if not isinstance(x, (int, float)) and x is not None]

    def mm(out, lhsT, rhs, start=True, stop=True):
        return P.op("pe", lambda e: e.matmul(out, lhsT=lhsT, rhs=rhs, start=start, stop=stop),
                    reads=[lhsT, rhs], writes=[out])

    def act(out, in_, func, scale=1.0, bias=0.0, eng="act"):
        return P.op(eng, lambda e: e.activation(out=out, in_=in_, func=func, bias=bias, scale=scale),
                    reads=aps(in_, scale, bias), writes=[out])

    def tt(out, a, b, op, eng="dve"):
        return P.op(eng, lambda e: e.tensor_tensor(out=out, in0=a, in1=b, op=op), reads=[a, b], writes=[out])

    def ts(out, a, s1, op0, s2=None, op1=None, eng="dve"):
        if op1 is None:
            return P.op(eng, lambda e: e.tensor_scalar(out=out, in0=a, scalar1=s1, scalar2=None, op0=op0),
                        reads=aps(a, s1), writes=[out])
        return P.op(eng, lambda e: e.tensor_scalar(out=out, in0=a, scalar1=s1, scalar2=s2, op0=op0, op1=op1),
                    reads=aps(a, s1, s2), writes=[out])

    def stt(out, in0, scalar, in1, op0, op1, eng="dve"):
        return P.op(eng, lambda e: e.scalar_tensor_tensor(out=out, in0=in0, scalar=scalar, in1=in1, op0=op0, op1=op1),
                    reads=aps(in0, scalar, in1), writes=[out])

    def cp(out, in_, eng="dve"):
        if eng == "act":
            return act(out, in_, AF.Copy)
        return P.op(eng, lambda e: e.tensor_copy(out=out, in_=in_), reads=[in_], writes=[out])

    def recip(out, in_):
        return P.op("dve", lambda e: e.reciprocal(out=out, in_=in_), reads=[in_], writes=[out])

    def memset(t, v, eng="pool"):
        return P.op(eng, lambda e: e.memset(t, v), writes=[t])

    def dma(out, in_, q="sp"):
        return P.op(q, lambda e: e.dma_start(out=out, in_=in_), reads=[in_], writes=[out], dma=True)

    final_ops = []
    glob = contextlib.ExitStack()
    with glob:
        psb = [P.psum(glob, f"ps{i}", [128, 512], F32) for i in range(8)]
        rot = {"i": 0, "banks": list(range(8))}

        def nps():
            b = rot["banks"][rot["i"] % len(rot["banks"])]
            rot["i"] += 1
            return psb[b]

        colp = P.sbuf(glob, "colp", [128, NCOLP], F32)
        ones_bf = P.sbuf(glob, "ones_bf", [128, 128], BF16)
        ones_f = P.sbuf(glob, "ones_f", [128, 128], F32)
        ident_bf = P.sbuf(glob, "ident_bf", [128, 128], BF16)
        selA = P.sbuf(glob, "selA", [2, 128], F32)
        selB = P.sbuf(glob, "selB", [2, 128], F32)
        osel = P.sbuf(glob, "osel", [128, 4], BF16)
        maskT = P.sbuf(glob, "maskT", [128, 128], F32)
        biasT = P.sbuf(glob, "biasT", [128, 4, 256], F32)
        lam = P.sbuf(glob, "lam", [128, 3 * L], F32)
        epsc = P.sbuf(glob, "epsc", [128, 1], F32)

        dma(colp[:], colp_d)
        memset(ones_bf[:], 1.0)
        memset(ones_f[:], 1.0)
        memset(epsc[:], EPS)
        memset(ident_bf[:], 1.0)
        P.op("pool", lambda e: e.affine_select(out=ident_bf[:], in_=ident_bf[:], compare_op=ALU.is_equal, fill=0.0,
                                               base=0, pattern=[[-1, 128]], channel_multiplier=1),
             reads=[ident_bf[:]], writes=[ident_bf[:]])
        memset(selA[:], 1.0)
        memset(selB[:], 1.0)
        P.op("pool", lambda e: e.affine_select(out=selA[:], in_=selA[:], compare_op=ALU.is_equal, fill=0.0,
                                               base=0, pattern=[[0, 128]], channel_multiplier=1),
             reads=[selA[:]], writes=[selA[:]])
        P.op("pool", lambda e: e.affine_select(out=selB[:], in_=selB[:], compare_op=ALU.is_equal, fill=0.0,
                                               base=-1, pattern=[[0, 128]], channel_multiplier=1),
             reads=[selB[:]], writes=[selB[:]])
        memset(osel[:], 0.0)
        memset(osel[:, 0:1], 1.0)
        memset(osel[:, 3:4], 1.0)

        with contextlib.ExitStack() as st:
            io = P.sbuf(st, "io", [128, 256], I32)
            dd = P.sbuf(st, "dd", [128, 256], F32)
            bk = P.sbuf(st, "bk", [128, 256], F32)
            eq = P.sbuf(st, "eq", [128, 256], F32)
            tb = P.sbuf(st, "tb", [128, 32, 4], F32)
            lv = P.sbuf(st, "lv", [128, 4, L, 64], F32)
            pr = P.sbuf(st, "pr", [128, 2, L, 64], F32)
            e12 = P.sbuf(st, "e12", [128, 2, L], F32)
            P.op("pool", lambda e: e.iota(io[:], pattern=[[1, 256]], base=0, channel_multiplier=-1), writes=[io[:]])
            cp(dd[:], io[:])
            ts(maskT[:], dd[:, 0:128], 0.0, ALU.is_lt, -30000.0, ALU.mult)
            ts(bk[:], dd[:], 0.0, ALU.max, 16.0, ALU.min)
            for thr in T5_THR:
                stt(bk[:], dd[:], float(thr), bk[:], ALU.is_ge, ALU.add)
            dma(tb[:].rearrange("p a b -> p (a b)"), relb_d.partition_broadcast(128))
            tdf = P.sbuf(st, "tdf", [128, 32, 4], F32)
            tt(tdf[:], tb[:], tb[:, 31:32, :].to_broadcast([128, 32, 4]), ALU.subtract)
            for h in range(4):
                ts(biasT[:, h, :], dd[:], 0.0, ALU.is_lt, -30000.0, ALU.mult)
            for b in range(31):
                ts(eq[:], bk[:], float(b), ALU.is_equal)
                for h in range(4):
                    stt(biasT[:, h, :], eq[:], tdf[:, b, h:h + 1], biasT[:, h, :], ALU.mult, ALU.add)
            dma(lv[:].rearrange("p a l d -> p (a l d)"), lamv_d.partition_broadcast(128))
            tt(pr[:, 0], lv[:, 0], lv[:, 1], ALU.mult)
            tt(pr[:, 1], lv[:, 2], lv[:, 3], ALU.mult)
            P.op("dve", lambda e: e.reduce_sum(out=e12[:].rearrange("p a l -> p (a l)"),
                                               in_=pr[:].rearrange("p a l d -> p (a l) d"), axis=AX.X),
                 reads=[pr[:]], writes=[e12[:]])
            act(e12[:], e12[:], AF.Exp)
            tt(lam[:, 0:L], e12[:, 0, :], e12[:, 1, :], ALU.subtract)
            for l in range(L):
                li = 0.8 - 0.6 * math.exp(-0.3 * l)
                ts(lam[:, l:l + 1], lam[:, l:l + 1], float(li), ALU.add)
                ts(lam[:, L + l:L + l + 1], lam[:, l:l + 1], -1.0, ALU.mult)
                c = cpi("diff_sub_norm", l, 0)
                ts(lam[:, 2 * L + l:2 * L + l + 1], colp[:, c:c + 1], float(1.0 - li), ALU.mult)
        with contextlib.ExitStack() as st:
            posi = P.sbuf(st, "posi", [128, S], I32)
            ang = P.sbuf(st, "ang", [128, S], F32)
            t1 = P.sbuf(st, "rt1", [128, S], F32)
            ki = P.sbuf(st, "rki", [128, S], I32)
            R = slice(64, 96)
            fc = cpi("freq", 0, 0)
            dma(posi[R, :], pos_d.partition_broadcast(32))
            cp(ang[R, :], posi[R, :])
            ts(ang[R, :], ang[R, :], colp[R, fc:fc + 1], ALU.mult)
            ts(t1[R, :], ang[R, :], float(1.0 / (2 * math.pi)), ALU.mult)
            cp(ki[R, :], t1[R, :])
            cp(t1[R, :], ki[R, :])
            stt(ang[R, :], t1[R, :], float(-2 * math.pi), ang[R, :], ALU.mult, ALU.add)
            for which, shift in ((1, 0.0), (0, math.pi / 2)):
                ts(t1[R, :], ang[R, :], float(shift), ALU.add)
                for _ in range(2):
                    ts(ki[R, :].bitcast(F32), t1[R, :], float(math.pi), ALU.is_gt, float(-2 * math.pi), ALU.mult)
                    tt(t1[R, :], t1[R, :], ki[R, :].bitcast(F32), ALU.add)
                ts(ki[R, :].bitcast(F32), t1[R, :], float(-math.pi), ALU.is_lt, float(2 * math.pi), ALU.mult)
                tt(t1[R, :], t1[R, :], ki[R, :].bitcast(F32), ALU.add)
                act(t1[R, :], t1[R, :], AF.Sin)
                dma(rope_d[which * 32:(which + 1) * 32, :], t1[R, :])

        xsrc = xin if first else xres
        if not first:
            pass

        def xview(t):
            return t.rearrange("(c p) s -> p c s", p=128)

        def norm_cast(xt, gname, l, hT, sq, rstd_b, rcol=None):
            act(sq[:].rearrange("p c t -> p (c t)"), xt[:].rearrange("p c t -> p (c t)"), AF.Square)
            ps = nps()
            for c in range(8):
                mm(ps[:, :], ones_bf[:, :], sq[:, c, :], start=(c == 0), stop=(c == 7))
            act(rstd_b[:], ps[:, :], AF.Sqrt, scale=1.0 / D, bias=epsc[:, 0:1])
            recip(rstd_b[:], rstd_b[:])
            if rcol is not None:
                ps2 = nps()
                for t in range(4):
                    for c in range(8):
                        mm(ps2[:, t:t + 1], sq[:, c, t * 128:(t + 1) * 128], ones_bf[:, 0:1], start=(c == 0), stop=(c == 7))
                act(rcol[:], ps2[:, 0:4], AF.Sqrt, scale=1.0 / D, bias=epsc[:, 0:1])
                recip(rcol[:], rcol[:])
            for c in range(8):
                gi = cpi(gname, l, c) if l is not None else COLP_OFF[gname][0] + c
                if c % 2 == 0:
                    ts(hT[:, c, :], xt[:, c, :], colp[:, gi:gi + 1], ALU.mult)
                else:
                    act(hT[:, c, :], xt[:, c, :], AF.Identity, scale=colp[:, gi:gi + 1])

        for li_, l in enumerate(layers):
            lam_init = 0.8 - 0.6 * math.exp(-0.3 * l)
            with contextlib.ExitStack() as st:
                cb = [P.sbuf(st, f"cb{i}", [128, BLK], BF16) for i in range(4)]
                for b in range(NBLK):
                    t = cb[b % 4]
                    dma(t[:], ws_d[li_ * NBLK + b], q="pool")
                    dma(wsb_d[b], t[:], q="sp")

            with contextlib.ExitStack() as stV:
                Vm = P.sbuf(stV, "Vm", [128, 32, 8, 65], BF16)
                Vd = P.sbuf(stV, "Vd", [128, 32, 512], BF16)
                memset(Vm[:].rearrange("p a h d -> p (a h) d")[:, :, 64:65], 1.0)
                with contextlib.ExitStack() as st:
                    w1 = P.sbuf(st, "w1", [128, 8, 1952], BF16)
                    wkr = P.sbuf(st, "wkr", [128, 8, 2, 96], BF16)
                    wuq = P.sbuf(st, "wuq", [128, 2, 8, 96], BF16)
                    wqr = P.sbuf(st, "wqr", [128, 2, 8, 96], BF16)
                    wukv = P.sbuf(st, "wukv", [128, 8, 128], BF16)
                    xt = P.sbuf(st, "xt", [128, 8, TG], F32)
                    hT = P.sbuf(st, "hT", [128, 8, TG], BF16)
                    sq = P.sbuf(st, "sq", [128, 8, TG], BF16)
                    rstd_b = P.sbuf(st, "rstd_b", [128, TG], F32)
                    rcol = P.sbuf(st, "rcol", [128, 4], F32)
                    cqf = P.sbuf(st, "cqf", [128, 2, TG], F32)
                    cqb = P.sbuf(st, "cqb", [128, 2, TG], BF16)
                    sqq = P.sbuf(st, "sqq", [128, 2, TG], BF16)
                    rq_b = P.sbuf(st, "rq_b", [128, TG], F32)
                    ckf = P.sbuf(st, "ckf", [128, TG], F32)
                    ckb = P.sbuf(st, "ckb", [128, TG], BF16)
                    sqk = P.sbuf(st, "sqk", [128, TG], BF16)
                    rk_b = P.sbuf(st, "rk_b", [128, TG], F32)
                    rkcol = P.sbuf(st, "rkcol", [128, 4], F32)
                    cs = P.sbuf(st, "cs", [128, TG], F32)
                    sn = P.sbuf(st, "sn", [128, TG], F32)
                    ta = P.sbuf(st, "ta", [128, TG], F32)
                    tb2 = P.sbuf(st, "tb2", [128, TG], F32)
                    kpe = P.sbuf(st, "kpe", [128, TG], BF16)
                    QTg = P.sbuf(st, "QTg", [96, 8, TG], BF16)
                    KTg = P.sbuf(st, "KTg", [96, 8, TG], BF16)
                    DQg = P.sbuf(st, "DQg", [128, 4, TG], BF16)
                    DKg = P.sbuf(st, "DKg", [128, 4, TG], BF16)

                    dma(w1[:].rearrange("p c n -> p (c n)"), w1_d[li_], q="pool")
                    dma(wuq[:].rearrange("p c h n -> p (c h n)"), wuq_d[li_], q="pool")
                    dma(wukv[:].rearrange("p h n -> p (h n)"), wukv_d[li_], q="pool")
                    memset(wkr[:], 0.0)
                    memset(wqr[:], 0.0)
                    for c in range(8):
                        cp(wkr[:, c, 0, 64:96], w1[:, c, 384:416], eng="pool")
                        ts(wkr[:, c, 1, 64:80], w1[:, c, 400:416], -1.0, ALU.mult, eng="pool")
                        cp(wkr[:, c, 1, 80:96], w1[:, c, 384:400], eng="pool")
                    for c in range(2):
                        ts(wqr[:, c, :, 64:80], wuq[:, c, :, 80:96], -1.0, ALU.mult, eng="pool")
                        cp(wqr[:, c, :, 80:96], wuq[:, c, :, 64:80], eng="pool")

                    for tg in range(NTG):
                        tsl = slice(tg * TG, (tg + 1) * TG)
                        dma(xt[:], xview(xsrc)[:, :, tsl])
                        dma(cs[64:96, :], rope_d[0:32, tsl])
                        dma(sn[64:96, :], rope_d[32:64, tsl])
                        norm_cast(xt, "norm_mix", l, hT, sq, rstd_b, rcol)
                        for cc in range(2):
                            ps = nps()
                            for c in range(8):
                                mm(ps[:, :], w1[:, c, cc * 128:(cc + 1) * 128], hT[:, c, :], start=(c == 0), stop=(c == 7))
                            tt(cqf[:, cc, :], ps[:, :], rstd_b[:], ALU.mult)
                        act(sqq[:].rearrange("p c t -> p (c t)"), cqf[:].rearrange("p c t -> p (c t)"), AF.Square)
                        ps = nps()
                        for cc in range(2):
                            mm(ps[:, :], ones_bf[:, :], sqq[:, cc, :], start=(cc == 0), stop=(cc == 1))
                        act(rq_b[:], ps[:, :], AF.Sqrt, scale=1.0 / 256, bias=epsc[:, 0:1])
                        recip(rq_b[:], rq_b[:])
                        for cc in range(2):
                            gi = cpi("mla_q_norm", l, cc)
                            ts(cqb[:, cc, :], cqf[:, cc, :], colp[:, gi:gi + 1], ALU.mult)
                        for h in range(8):
                            pa = nps()
                            pb = nps()
                            for cc in range(2):
                                mm(pa[0:96, :], wuq[:, cc, h, :], cqb[:, cc, :], start=(cc == 0), stop=(cc == 1))
                            for cc in range(2):
                                mm(pb[0:96, :], wqr[:, cc, h, :], cqb[:, cc, :], start=(cc == 0), stop=(cc == 1))
                            tt(QTg[0:64, h, :], pa[0:64, :], rq_b[0:64, :], ALU.mult)
                            tt(ta[64:96, :], pa[64:96, :], cs[64:96, :], ALU.mult)
                            tt(tb2[64:96, :], pb[64:96, :], sn[64:96, :], ALU.mult)
                            tt(ta[64:96, :], ta[64:96, :], tb2[64:96, :], ALU.add)
                            tt(QTg[64:96, h, :], ta[64:96, :], rq_b[64:96, :], ALU.mult)
                        dma(QT_d.rearrange("p (h s) -> p h s", h=8)[:, :, tsl], QTg[:])
                        ps = nps()
                        for c in range(8):
                            mm(ps[:, :], w1[:, c, 256:384], hT[:, c, :], start=(c == 0), stop=(c == 7))
                        tt(ckf[:], ps[:, :], rstd_b[:], ALU.mult)
                        act(sqk[:], ckf[:], AF.Square)
                        ps = nps()
                        mm(ps[:, :], ones_bf[:, :], sqk[:], start=True, stop=True)
                        act(rk_b[:], ps[:, :], AF.Sqrt, scale=1.0 / 128, bias=epsc[:, 0:1])
                        recip(rk_b[:], rk_b[:])
                        ps2 = nps()
                        for t in range(4):
                            mm(ps2[:, t:t + 1], sqk[:, t * 128:(t + 1) * 128], ones_bf[:, 0:1], start=True, stop=True)
                        act(rkcol[:], ps2[:, 0:4], AF.Sqrt, scale=1.0 / 128, bias=epsc[:, 0:1])
                        recip(rkcol[:], rkcol[:])
                        gi = cpi("mla_kv_norm", l, 0)
                        ts(ckb[:], ckf[:], colp[:, gi:gi + 1], ALU.mult)
                        for h in range(8):
                            pk = nps()
                            mm(pk[0:64, :], wukv[:, h, 0:64], ckb[:], start=True, stop=True)
                            tt(KTg[0:64, h, :], pk[0:64, :], rk_b[0:64, :], ALU.mult)
                        for t in range(4):
                            pv = nps()
                            mm(pv[:, :].rearrange("p (h d) -> p h d", h=8), ckb[:, t * 128:(t + 1) * 128], wukv[:, :, 64:128],
                               start=True, stop=True)
                            ts(Vm[:, tg * 4 + t, :, 0:64], pv[:, :].rearrange("p (h d) -> p h d", h=8), rkcol[:, t:t + 1], ALU.mult)
                        pa = nps()
                        pb = nps()
                        for c in range(8):
                            mm(pa[0:96, :], wkr[:, c, 0, :], hT[:, c, :], start=(c == 0), stop=(c == 7))
                        for c in range(8):
                            mm(pb[0:96, :], wkr[:, c, 1, :], hT[:, c, :], start=(c == 0), stop=(c == 7))
                        tt(ta[64:96, :], pa[64:96, :], cs[64:96, :], ALU.mult)
                        tt(tb2[64:96, :], pb[64:96, :], sn[64:96, :], ALU.mult)
                        tt(ta[64:96, :], ta[64:96, :], tb2[64:96, :], ALU.add)
                        tt(kpe[64:96, :], ta[64:96, :], rstd_b[64:96, :], ALU.mult)
                        cp(KTg[64:96, :, :], kpe[64:96, :].unsqueeze(1).to_broadcast([32, 8, TG]), eng="pool")
                        dma(KT_d.rearrange("p (h s) -> p h s", h=8)[:, :, tsl], KTg[:])
                        for (dst, c0) in ((DQg, 416), (DKg, 928)):
                            for cc in range(4):
                                ps = nps()
                                for c in range(8):
                                    mm(ps[:, :], w1[:, c, c0 + cc * 128:c0 + (cc + 1) * 128], hT[:, c, :], start=(c == 0), stop=(c == 7))
                                tt(dst[:, cc, :], ps[:, :], rstd_b[:], ALU.mult)
                        dma(DQ_d.rearrange("p (h s) -> p h s", h=4)[:, :, tsl], DQg[:])
                        dma(DK_d.rearrange("p (h s) -> p h s", h=4)[:, :, tsl], DKg[:])
                        for t in range(4):
                            pv = nps()
                            for c in range(8):
                                mm(pv[:, :], hT[:, c, t * 128:(t + 1) * 128], w1[:, c, 1440:1952], start=(c == 0), stop=(c == 7))
                            ts(Vd[:, tg * 4 + t, :], pv[:, :], rcol[:, t:t + 1], ALU.mult)

                with contextlib.ExitStack() as st:
                    Qh = [P.sbuf(st, f"Qh{i}", [128, S], BF16) for i in range(2)]
                    Kh = [P.sbuf(st, f"Kh{i}", [128, S], BF16) for i in range(2)]
                    pts = [P.sbuf(st, f"pt{i}", [128, TG], BF16) for i in range(6)]
                    tmps = [P.sbuf(st, f"tmp{i}", [128, 256], F32) for i in range(3)]
                    rd = P.sbuf(st, "rd", [128, TG], F32)
                    bcs = [P.sbuf(st, f"bcs{i}", [128, TG], F32) for i in range(2)]
                    t0 = P.sbuf(st, "t0", [128, TG], F32)
                    t1 = P.sbuf(st, "t1", [128, TG], F32)
                    od = P.sbuf(st, "od", [128, TG], F32)
                    sqo = P.sbuf(st, "sqo", [128, TG], BF16)
                    rso = P.sbuf(st, "rso", [128, TG], F32)
                    ohs = [P.sbuf(st, f"oh{i}", [128, TG], BF16) for i in range(2)]
                    cnt = {"pt": 0, "tmp": 0, "oh": 0}
                    rot["banks"] = [3, 4, 5, 6, 7]
                    units = [("m", h) for h in range(8)] + [("d", h) for h in range(4)]

                    def load_unit(u, i):
                        kind, h = u
                        if kind == "m":
                            dma(Qh[i][0:96, :], QT_d[:, h * S:(h + 1) * S])
                            dma(Kh[i][0:96, :], KT_d[:, h * S:(h + 1) * S])
                        else:
                            dma(Qh[i][:, :], DQ_d[:, h * S:(h + 1) * S])
                            dma(Kh[i][:, :], DK_d[:, h * S:(h + 1) * S])

                    load_unit(units[0], 0)
                    for ui, u in enumerate(units):
                        kind, h = u
                        if ui + 1 < len(units):
                            load_unit(units[ui + 1], (ui + 1) % 2)
                        Q = Qh[ui % 2]
                        K = Kh[ui % 2]
                        for g in range(NTG):
                            nk = 4 * g + 4
                            if kind == "m":
                                scale = 96 ** -0.5
                                oacc = psb[g % 2]
                                for kt in range(nk):
                                    j = kt - 4 * g
                                    lo = 128 * j if j > 0 else 0
                                    ps = nps()
                                    mm(ps[:, lo:TG], K[0:96, kt * 128:(kt + 1) * 128], Q[0:96, g * TG + lo:(g + 1) * TG])
                                    pt = pts[cnt["pt"] % len(pts)]
                                    cnt["pt"] += 1
                                    if j >= 0:
                                        tm = tmps[cnt["tmp"] % len(tmps)]
                                        cnt["tmp"] += 1
                                        stt(tm[:, 0:128], ps[:, lo:lo + 128], float(scale), maskT[:], ALU.mult, ALU.add)
                                        act(pt[:, lo:lo + 128], tm[:, 0:128], AF.Exp)
                                        if lo + 128 < TG:
                                            act(pt[:, lo + 128:TG], ps[:, lo + 128:TG], AF.Exp, scale=float(scale))
                                    else:
                                        act(pt[:, :], ps[:, :], AF.Exp, scale=float(scale))
                                    mm(oacc[0:65, lo:TG], Vm[:, kt, h, :], pt[:, lo:TG], start=(kt == 0), stop=(kt == nk - 1))
                                recip(rd[64:65, :], oacc[64:65, :])
                                bp = nps()
                                mm(bp[0:64, :], ones_f[64:65, 0:64], rd[64:65, :])
                                bc = bcs[g % 2]
                                act(bc[0:64, :], bp[0:64, :], AF.Copy)
                                oh = ohs[cnt["oh"] % 2]
                                cnt["oh"] += 1
                                tt(oh[0:64, :], oacc[0:64, :], bc[0:64, :], ALU.mult)
                                dma(oB_d[h * 64:(h + 1) * 64, g * TG:(g + 1) * TG], oh[0:64, :])
                            else:
                                scale = 64 ** -0.5
                                oa = (psb[0], psb[1])
                                den = psb[2]
                                for kt in range(nk):
                                    j = kt - 4 * g
                                    lo = 128 * j if j > 0 else 0
                                    if j >= 0:
                                        b0, b1, bo = lo, min(lo + 256, TG), 0
                                    elif j == -1:
                                        b0, b1, bo = 0, 128, 128
                                    else:
                                        b0 = b1 = bo = 0
                                    for m in range(2):
                                        ps = nps()
                                        mm(ps[:, lo:TG], K[64 * m:64 * m + 64, kt * 128:(kt + 1) * 128],
                                           Q[64 * m:64 * m + 64, g * TG + lo:(g + 1) * TG])
                                        pt = pts[cnt["pt"] % len(pts)]
                                        cnt["pt"] += 1
                                        if b1 > b0:
                                            tm = tmps[cnt["tmp"] % len(tmps)]
                                            cnt["tmp"] += 1
                                            w = b1 - b0
                                            stt(tm[:, 0:w], ps[:, b0:b1], float(scale), biasT[:, h, bo:bo + w], ALU.mult, ALU.add)
                                            act(pt[:, b0:b1], tm[:, 0:w], AF.Exp)
                                            if b1 < TG:
                                                act(pt[:, b1:TG], ps[:, b1:TG], AF.Exp, scale=float(scale))
                                        else:
                                            act(pt[:, :], ps[:, :], AF.Exp, scale=float(scale))
                                        mm(oa[m][:, lo:TG], Vd[:, kt, h * 128:(h + 1) * 128], pt[:, lo:TG],
                                           start=(kt == 0), stop=(kt == nk - 1))
                                        mm(den[0:2, lo:TG], osel[:, 2 * m:2 * m + 2], pt[:, lo:TG],
                                           start=(kt == 0 and m == 0), stop=(kt == nk - 1 and m == 1))
                                recip(rd[0:2, :], den[0:2, :])
                                for m, sel in ((0, selA), (1, selB)):
                                    bp = nps()
                                    mm(bp[:, :], sel[0:2, :], rd[0:2, :])
                                    act(bcs[m][:, :], bp[:, :], AF.Copy)
                                tt(t0[:], oa[0][:, :], bcs[0][:, :], ALU.mult)
                                tt(t1[:], oa[1][:, :], bcs[1][:, :], ALU.mult)
                                stt(od[:], t1[:], lam[:, L + l:L + l + 1], t0[:], ALU.mult, ALU.add)
                                act(sqo[:], od[:], AF.Square)
                                sp_ = nps()
                                mm(sp_[:, :], ones_bf[:, :], sqo[:])
                                act(rso[:], sp_[:, :], AF.Sqrt, scale=1.0 / 128, bias=epsc[:, 0:1])
                                recip(rso[:], rso[:])
                                tt(od[:], od[:], rso[:], ALU.mult)
                                oh = ohs[cnt["oh"] % 2]
                                cnt["oh"] += 1
                                act(oh[:, :], od[:], AF.Identity, scale=lam[:, 2 * L + l:2 * L + l + 1])
                                dma(oC_d[h * 128:(h + 1) * 128, g * TG:(g + 1) * TG], oh[:, :])
                    rot["banks"] = list(range(8))

            with contextlib.ExitStack() as st:
                NR = 5
                ring = [P.sbuf(st, f"wr{i}", [128, BLK], BF16) for i in range(NR)]
                diag = P.sbuf(st, "diag", [128, 124, 128], BF16)
                xt = P.sbuf(st, "xt3", [128, 8, TG], F32)
                hT = P.sbuf(st, "hT3", [128, 8, TG], BF16)
                sq = P.sbuf(st, "sq3", [128, 8, TG], BF16)
                rstd_b = P.sbuf(st, "rstd3", [128, TG], F32)
                aT = P.sbuf(st, "aT", [128, 4, 30 + TG], BF16)
                wk = [P.sbuf(st, f"wk{i}", [128, TG], F32) for i in range(4)]
                halo = P.sbuf(st, "halo", [128, 44, 2], F32)
                cnt3 = {"wk": 0, "sb": 0, "ub": 0, "blk": 0}

                def nwk():
                    cnt3["wk"] += 1
                    return wk[cnt3["wk"] % 4]

                for j in range(31):
                    for c in range(4):
                        ci = cpi("conv_w", l, j * 4 + c)
                        ts(diag[:, j * 4 + c, :], ident_bf[:], colp[:, ci:ci + 1], ALU.mult, eng="pool")
                memset(aT[:, :, 0:30], 0.0)
                memset(halo[:], 0.0)

                stream = {"next": 0}
                total_blocks = NTG * NBLK

                def prefetch():
                    n = stream["next"]
                    if n < total_blocks:
                        dma(ring[n % NR][:], wsb_d[n % NBLK])
                        stream["next"] = n + 1

                def getblk(k):
                    while stream["next"] <= min(k + 2, total_blocks - 1):
                        prefetch()
                    return ring[k % NR]

                kblk = 0
                for tg in range(NTG):
                    tsl = slice(tg * TG, (tg + 1) * TG)
                    dma(xt[:], xview(xsrc)[:, :, tsl])
                    stA = contextlib.ExitStack()
                    af = P.sbuf(stA, "af", [128, 4, TG], F32)
                    afb = P.sbuf(stA, "afb", [128, 4, TG], BF16)
                    sqa = P.sbuf(stA, "sqa", [128, 4, TG], BF16)
                    mu = P.sbuf(stA, "mu", [128, TG], F32)
                    rl = P.sbuf(stA, "rl", [128, TG], F32)
                    sT = P.sbuf(stA, "sT", [128, 4, TG], BF16)
                    oBt = P.sbuf(stA, "oBt", [128, 4, TG], BF16)
                    oCt = P.sbuf(stA, "oCt", [128, 4, TG], BF16)
                    macc = P.sbuf(stA, "macc", [128, 8, TG], F32)
                    mT = P.sbuf(stA, "mT", [128, 8, TG], BF16)
                    dma(oBt[:], oB_d.rearrange("(c p) s -> p c s", p=128)[:, :, tsl])
                    dma(oCt[:], oC_d.rearrange("(c p) s -> p c s", p=128)[:, :, tsl])
                    norm_cast(xt, "norm_mix", l, hT, sq, rstd_b)
                    for jb in range(2):
                        wb = getblk(kblk).rearrange("p (c n) -> p c n", c=8)
                        kblk += 1
                        for i in range(2):
                            pv = nps()
                            pg = nps()
                            for c in range(8):
                                mm(pv[:, :], wb[:, c, i * 128:(i + 1) * 128], hT[:, c, :], start=(c == 0), stop=(c == 7))
                            for c in range(8):
                                mm(pg[:, :], wb[:, c, 256 + i * 128:256 + (i + 1) * 128], hT[:, c, :], start=(c == 0), stop=(c == 7))
                            w0 = nwk()
                            w1_ = nwk()
                            tt(w0[:], pg[:, :], rstd_b[:], ALU.mult)
                            act(w0[:], w0[:], AF.Sigmoid)
                            tt(w1_[:], pv[:, :], rstd_b[:], ALU.mult)
                            tt(aT[:, jb * 2 + i, 30:30 + TG], w1_[:], w0[:], ALU.mult)
                    for c in range(4):
                        pc = nps()
                        for j in range(31):
                            mm(pc[:, :], diag[:, j * 4 + c, :], aT[:, c, j:j + TG], start=(j == 0), stop=(j == 30))
                        bi = cpi("conv_b", l, c)
                        act(af[:, c, :], pc[:, :], AF.Identity, bias=colp[:, bi:bi + 1])
                        act(sqa[:, c, :], pc[:, :], AF.Square, bias=colp[:, bi:bi + 1])
                        cp(afb[:, c, :], af[:, c, :], eng="pool")
                        cp(aT[:, c, 0:30], aT[:, c, TG:TG + 30], eng="pool")
                    p1 = nps()
                    p2 = nps()
                    for c in range(4):
                        mm(p1[:, :], ones_bf[:, :], afb[:, c, :], start=(c == 0), stop=(c == 3))
                    for c in range(4):
                        mm(p2[:, :], ones_bf[:, :], sqa[:, c, :], start=(c == 0), stop=(c == 3))
                    ts(mu[:], p1[:, :], 1.0 / 512, ALU.mult)
                    w0 = nwk()
                    tt(w0[:], mu[:], mu[:], ALU.mult)
                    stt(rl[:], p2[:, :], 1.0 / 512, w0[:], ALU.mult, ALU.subtract)
                    act(rl[:], rl[:], AF.Sqrt, bias=epsc[:, 0:1])
                    recip(rl[:], rl[:])
                    for c in range(4):
                        w0 = nwk()
                        tt(w0[:], af[:, c, :], mu[:], ALU.subtract)
                        tt(w0[:], w0[:], rl[:], ALU.mult)
                        gi = cpi("conv_ln_g", l, c)
                        bi = cpi("conv_ln_b", l, c)
                        act(sT[:, c, :], w0[:], AF.Silu, scale=colp[:, gi:gi + 1], bias=colp[:, bi:bi + 1])
                    for br, src in enumerate((sT, oBt, oCt)):
                        wo_ = getblk(kblk).rearrange("p (c n) -> p c n", c=4)
                        kblk += 1
                        for half in range(2):
                            wg = getblk(kblk).rearrange("p (c n) -> p c n", c=8)
                            kblk += 1
                            for o4 in range(4):
                                oc = half * 4 + o4
                                py = nps()
                                pg = nps()
                                for c in range(4):
                                    mm(py[:, :], wo_[:, c, oc * 128:(oc + 1) * 128], src[:, c, :], start=(c == 0), stop=(c == 3))
                                for c in range(8):
                                    mm(pg[:, :], wg[:, c, o4 * 128:(o4 + 1) * 128], hT[:, c, :], start=(c == 0), stop=(c == 7))
                                w0 = nwk()
                                tt(w0[:], pg[:, :], rstd_b[:], ALU.mult)
                                bi = cpi("gate_bias", l, br * 8 + oc)
                                act(w0[:], w0[:], AF.Sigmoid, bias=colp[:, bi:bi + 1])
                                if br == 0:
                                    tt(macc[:, oc, :], w0[:], py[:, :], ALU.mult)
                                else:
                                    tt(w0[:], w0[:], py[:, :], ALU.mult)
                                    if br == 1:
                                        tt(macc[:, oc, :], macc[:, oc, :], w0[:], ALU.add, eng="pool")
                                    else:
                                        tt(mT[:, oc, :], macc[:, oc, :], w0[:], ALU.add, eng="pool")
                    for half in range(2):
                        wo_ = getblk(kblk).rearrange("p (c n) -> p c n", c=8)
                        kblk += 1
                        for o4 in range(4):
                            oc = half * 4 + o4
                            po = nps()
                            for c in range(8):
                                mm(po[:, :], wo_[:, c, o4 * 128:(o4 + 1) * 128], mT[:, c, :], start=(c == 0), stop=(c == 7))
                            tt(xt[:, oc, :], xt[:, oc, :], po[:, :], ALU.add)
                    stA.close()
                    stB = contextlib.ExitStack()
                    sb_ = [P.sbuf(stB, f"sbf{i}", [128, 2 + TG], F32) for i in range(4)]
                    ub = [P.sbuf(stB, f"ub{i}", [128, TG], F32) for i in range(4)]
                    actT = P.sbuf(stB, "actT", [128, 22, TG], BF16)
                    norm_cast(xt, "norm_ffn", l, hT, sq, rstd_b)
                    for kb in range(11):
                        wu = getblk(kblk).rearrange("p (c n) -> p c n", c=8)
                        kblk += 1
                        for i in range(2):
                            pi = kb * 2 + i
                            us = []
                            for part in range(2):
                                ch = pi + 22 * part
                                pu = nps()
                                for c in range(8):
                                    mm(pu[:, :], wu[:, c, part * 256 + i * 128:part * 256 + (i + 1) * 128], hT[:, c, :],
                                       start=(c == 0), stop=(c == 7))
                                cnt3["sb"] += 1
                                s_ = sb_[cnt3["sb"] % 4]
                                cnt3["ub"] += 1
                                u_ = ub[cnt3["ub"] % 4]
                                cp(s_[:, 0:2], halo[:, ch, :], eng="pool")
                                tt(s_[:, 2:2 + TG], pu[:, :], rstd_b[:], ALU.mult)
                                cp(halo[:, ch, :], s_[:, TG:TG + 2], eng="pool")
                                k2 = cpi("ffn_conv_w", l, 2 * 44 + ch)
                                k1 = cpi("ffn_conv_w", l, 1 * 44 + ch)
                                k0 = cpi("ffn_conv_w", l, 0 * 44 + ch)
                                bi = cpi("ffn_conv_b", l, ch)
                                act(u_[:], s_[:, 2:2 + TG], AF.Identity, scale=colp[:, k2:k2 + 1], bias=colp[:, bi:bi + 1])
                                stt(u_[:], s_[:, 1:1 + TG], colp[:, k1:k1 + 1], u_[:], ALU.mult, ALU.add)
                                stt(u_[:], s_[:, 0:TG], colp[:, k0:k0 + 1], u_[:], ALU.mult, ALU.add)
                                us.append(u_)
                            act(us[0][:], us[0][:], AF.Silu)
                            tt(actT[:, pi, :], us[0][:], us[1][:], ALU.mult)
                    for oc in range(8):
                        wd = getblk(kblk).rearrange("p (c n) -> p c n", c=32)
                        kblk += 1
                        po = nps()
                        for c in range(22):
                            mm(po[:, :], wd[:, c, :], actT[:, c, :], start=(c == 0), stop=(c == 21))
                        tt(xt[:, oc, :], xt[:, oc, :], po[:, :], ALU.add)
                    dma(xview(xres)[:, :, tsl], xt[:])
                    stB.close()
                assert kblk == total_blocks
            xsrc = xres

        if final:
            with contextlib.ExitStack() as st:
                xt = P.sbuf(st, "xtf", [128, 8, TG], F32)
                sq = P.sbuf(st, "sqf", [128, 8, TG], BF16)
                rstd_b = P.sbuf(st, "rstdf", [128, TG], F32)
                yt = [P.sbuf(st, f"ytf{i}", [128, 8, TG], F32) for i in range(2)]
                for tg in range(NTG):
                    tsl = slice(tg * TG, (tg + 1) * TG)
                    dma(xt[:], xview(xsrc)[:, :, tsl])
                    act(sq[:].rearrange("p c t -> p (c t)"), xt[:].rearrange("p c t -> p (c t)"), AF.Square)
                    ps = nps()
                    for c in range(8):
                        mm(ps[:, :], ones_bf[:, :], sq[:, c, :], start=(c == 0), stop=(c == 7))
                    act(rstd_b[:], ps[:, :], AF.Sqrt, scale=1.0 / D, bias=epsc[:, 0:1])
                    recip(rstd_b[:], rstd_b[:])
                    y = yt[tg % 2]
                    for c in range(8):
                        gi = COLP_OFF["norm_final"][0] + c
                        stt(y[:, c, :], xt[:, c, :], colp[:, gi:gi + 1], rstd_b[:], ALU.mult, ALU.mult)
                    final_ops.append(dma(xview(yout)[:, :, tsl], y[:]))
        else:
            with contextlib.ExitStack() as st:
                xt = P.sbuf(st, "xtc", [128, 8, TG], F32)
                for tg in range(NTG):
                    tsl = slice(tg * TG, (tg + 1) * TG)
                    dma(xt[:], xview(xsrc)[:, :, tsl])
                    final_ops.append(dma(xview(yout)[:, :, tsl], xt[:]))
        P.emit(final_ops)
    return nc, P


LAYER_GROUPS = [[0, 1, 2, 3]]


def kernel(**inp):
    x = np.asarray(inp["x"], np.float32)
    B = x.shape[0]
    colp = host_colp(inp)
    w1, wuq, wukv, ws = host_weights(inp)
    relb = np.ascontiguousarray(np.asarray(inp["rel_bias"], np.float32).reshape(1, 128))
    lamv = np.ascontiguousarray(np.stack([np.asarray(inp[k], np.float32) for k in
                                          ("diff_lam_q1", "diff_lam_k1", "diff_lam_q2", "diff_lam_k2")]).reshape(1, -1))
    pos = np.asarray(inp["positions"], np.int32)
    cur = [np.ascontiguousarray(x[b].T) for b in range(B)]
    for gi, grp in enumerate(LAYER_GROUPS):
        nc, _ = build(grp, first=True, final=(gi == len(LAYER_GROUPS) - 1))
        in_maps = []
        for b in range(B):
            in_maps.append({
                "xin": cur[b], "pos": np.ascontiguousarray(pos[b][None, :]), "colp": colp, "relb": relb, "lamv": lamv,
                "w1": np.ascontiguousarray(w1[grp].reshape(len(grp), 128, -1)),
                "wuq": np.ascontiguousarray(wuq[grp].reshape(len(grp), 128, -1)),
                "wukv": np.ascontiguousarray(wukv[grp]),
                "ws": np.ascontiguousarray(ws[grp].reshape(len(grp) * NBLK, 128, BLK)),
            })
        res = run_bass_kernel_spmd(nc, in_maps, core_ids=list(range(B)))
        cur = [np.asarray(res.results[b]["yout"]) for b in range(B)]
    out = np.stack([c.T for c in cur]).astype(np.float32)
    return out
```

```python
import contextlib
import math

import numpy as np
import concourse.bass as bass
import concourse.mybir as mybir
from concourse.bass_utils import run_bass_kernel_spmd

F32 = mybir.dt.float32
BF16 = mybir.dt.bfloat16
I32 = mybir.dt.int32
ALU = mybir.AluOpType
AF = mybir.ActivationFunctionType
AX = mybir.AxisListType

D = 1024
S = 4096
L = 4
NTG = 8
TG = 512
EPS = 1e-6
D_FF = 2816
NBLK = 32
BLK = 4096

EPOCH = 8000
NDMASEM = 16


class Op:
    __slots__ = ("eng", "fn", "deps", "needs_inc", "dma", "cnt", "dsem", "dval")

    def __init__(self, eng, fn, dma):
        self.eng = eng
        self.fn = fn
        self.deps = []
        self.needs_inc = False
        self.dma = dma
        self.cnt = 0
        self.dsem = None
        self.dval = 0


class Prog:
    ENGS = ("pe", "act", "dve", "pool", "sp")
    BK = 2048

    def __init__(self, nc):
        self.nc = nc
        self.ops = {e: [] for e in self.ENGS}
        self.live = {}
        self.meta = {}
        self.nops = 0
        self._uid = 0

    def reg(self, t, rowsz, space=None, base=0, esz=1):
        self.meta[t.name] = (rowsz, space if space else t.name, base, esz)
        return t

    def sbuf(self, st, name, shape, dt):
        nc = self.nc
        addr = (nc.sbuf_base + 31) // 32 * 32
        self._uid += 1
        t = st.enter_context(nc.sbuf_tensor(f"sb{self._uid}_{name}", shape, dt))
        esz = 2 if dt == BF16 else 4
        return self.reg(t, int(np.prod(shape[1:])), "sb", addr, esz)

    def psum(self, st, name, shape, dt):
        nc = self.nc
        addr = nc.psum_base * 2048
        t = st.enter_context(nc.psum_tensor(name, shape, dt))
        return self.reg(t, int(np.prod(shape[1:])), "ps", addr, 4)

    def dram(self, name, shape, dt, kind):
        t = self.nc.dram_tensor(name, shape, dt, kind=kind)
        return self.reg(t, int(shape[-1]))

    def region(self, ap):
        C, space, base, esz = self.meta[ap.tensor.name]
        off = int(ap.offset)
        r0, c0 = divmod(off, C)
        re, ce = 0, 0
        for step, cnt in ap.ap:
            if cnt <= 1 or step == 0:
                continue
            if step % C == 0:
                re += (step // C) * (cnt - 1)
            else:
                ce += step * (cnt - 1)
        return (space, r0, r0 + re + 1, base + c0 * esz, base + (c0 + ce + 1) * esz)

    def _buckets(self, rg):
        if rg[0] in ("sb", "ps"):
            return [(rg[0], k) for k in range(rg[3] // self.BK, (rg[4] - 1) // self.BK + 1)]
        return [(rg[0], 0)]

    def op(self, eng, fn, reads=(), writes=(), dma=False):
        o = Op(eng, fn, dma)
        self.nops += 1
        deps = {}
        rregs = [self.region(a) for a in reads]
        wregs = [self.region(a) for a in writes]
        live = self.live
        for rg in rregs:
            for bk in self._buckets(rg):
                d = live.get(bk)
                if not d:
                    continue
                for ent in d.values():
                    if ent[5] and ent[1] < rg[2] and rg[1] < ent[2] and ent[3] < rg[4] and rg[3] < ent[4]:
                        deps[id(ent[6])] = ent[6]
        for rg in wregs:
            for bk in self._buckets(rg):
                d = live.get(bk)
                if not d:
                    continue
                dead = []
                for key, ent in d.items():
                    if ent[1] < rg[2] and rg[1] < ent[2] and ent[3] < rg[4] and rg[3] < ent[4]:
                        deps[id(ent[6])] = ent[6]
                        if rg[1] <= ent[1] and ent[2] <= rg[2] and rg[3] <= ent[3] and ent[4] <= rg[4]:
                            dead.append(key)
                for key in dead:
                    del d[key]
        for d in deps.values():
            if d is o:
                continue
            if d.eng == "pe" and eng == "pe" and not d.dma and not dma:
                continue
            d.needs_inc = True
            o.deps.append(d)
        for rg in rregs:
            if dma:
                self._uid += 1
                key = ("r", self._uid)
            else:
                key = ("r", eng, rg[1], rg[2], rg[3], rg[4])
            ent = (rg[0], rg[1], rg[2], rg[3], rg[4], False, o)
            for bk in self._buckets(rg):
                live.setdefault(bk, {})[key] = ent
        for rg in wregs:
            self._uid += 1
            key = ("w", self._uid)
            ent = (rg[0], rg[1], rg[2], rg[3], rg[4], True, o)
            for bk in self._buckets(rg):
                live.setdefault(bk, {})[key] = ent
        self.ops[eng].append(o)
        return o

    def emit(self, final_ops=()):
        nc = self.nc
        nsem_eng = {}
        for e in self.ENGS:
            c = 0
            j = 0
            for o in self.ops[e]:
                if o.dma:
                    o.dsem = (e, j % NDMASEM)
                    o.dval = 16 * (j // NDMASEM + 1)
                    j += 1
                elif o.needs_inc:
                    c += 1
                    o.cnt = c
            nsem_eng[e] = (c + EPOCH - 1) // EPOCH
        sems = {}
        stack = contextlib.ExitStack()
        for e in self.ENGS:
            for k in range(max(nsem_eng[e], 1)):
                sems[(e, "c", k)] = stack.enter_context(nc.semaphore(f"s_{e}_c{k}"))
            if any(o.dma for o in self.ops[e]):
                for k in range(NDMASEM):
                    sems[(e, "d", k)] = stack.enter_context(nc.semaphore(f"s_{e}_d{k}"))
        prog = self

        def sig(o):
            if o.dma:
                return (o.dsem[0], "d", o.dsem[1]), o.dval
            ep, loc = divmod(o.cnt - 1, EPOCH)
            return (o.eng, "c", ep), loc + 1

        def run_engine(e, eng):
            waited = {}

            def do_wait(key, val):
                if waited.get(key, 0) >= val:
                    return
                waited[key] = val
                eng.wait_ge(sems[key], val)

            for o in prog.ops[e]:
                need = {}
                for d in o.deps:
                    key, val = sig(d)
                    if need.get(key, 0) < val:
                        need[key] = val
                if o.dma and o.dval > 16:
                    key = (e, "d", o.dsem[1])
                    if need.get(key, 0) < o.dval - 16:
                        need[key] = o.dval - 16
                for key, val in need.items():
                    do_wait(key, val)
                ins = o.fn(eng)
                if o.dma:
                    ins.then_inc(sems[(e, "d", o.dsem[1])], 16)
                elif o.needs_inc:
                    ep, loc = divmod(o.cnt - 1, EPOCH)
                    ins.then_inc(sems[(e, "c", ep)], 1)
            if e == "sp":
                for o in final_ops:
                    key, val = sig(o)
                    do_wait(key, val)

        with stack:
            with nc.Block() as block:
                @block.tensor
                def _(eng):
                    run_engine("pe", eng)

                @block.scalar
                def _(eng):
                    run_engine("act", eng)

                @block.vector
                def _(eng):
                    run_engine("dve", eng)

                @block.gpsimd
                def _(eng):
                    run_engine("pool", eng)

                @block.sync
                def _(eng):
                    run_engine("sp", eng)


COLP_SPEC = [
    ("norm_mix", 8), ("gate_bias", 24), ("conv_b", 4), ("conv_ln_g", 4), ("conv_ln_b", 4),
    ("mla_q_norm", 2), ("mla_kv_norm", 1), ("diff_sub_norm", 1), ("norm_ffn", 8),
    ("ffn_conv_b", 44), ("conv_w", 31 * 4), ("ffn_conv_w", 3 * 44),
]
COLP_OFF = {}
_o = 0
for _n, _c in COLP_SPEC:
    COLP_OFF[_n] = (_o, _c)
    _o += L * _c
COLP_OFF["norm_final"] = (_o, 8)
_o += 8
COLP_OFF["freq"] = (_o, 1)
_o += 1
NCOLP = _o


def cpi(name, l, c):
    off, n = COLP_OFF[name]
    return off + l * n + c


def host_colp(inp):
    colp = np.zeros((128, NCOLP), np.float32)
    for name, n in COLP_SPEC:
        a = np.asarray(inp[name], np.float32).reshape(L, n, 128)
        off = COLP_OFF[name][0]
        colp[:, off:off + L * n] = a.reshape(L * n, 128).T
    off = COLP_OFF["norm_final"][0]
    colp[:, off:off + 8] = np.asarray(inp["norm_final"], np.float32).reshape(8, 128).T
    fr = np.float32(10000.0) ** (-np.arange(16, dtype=np.float32) / np.float32(16))
    colp[64:96, COLP_OFF["freq"][0]] = np.concatenate([fr, fr])
    return colp


def kmaj(w, kc):
    K, N = w.shape
    return np.ascontiguousarray(w.reshape(kc, 128, N).transpose(1, 0, 2))


def host_weights(inp):
    w_in = np.asarray(inp["w_in"], np.float32)
    w1 = np.stack([kmaj(w_in[l][:, 1024:2976], 8) for l in range(L)])
    wuq = np.stack([kmaj(np.asarray(inp["w_uq"][l], np.float32), 2) for l in range(L)])
    wukv = np.ascontiguousarray(np.asarray(inp["w_ukv"], np.float32))
    ws = np.zeros((L, NBLK, 128, BLK), np.float32)
    for l in range(L):
        b = 0
        wi = w_in[l]
        for j in range(2):
            cols = np.concatenate([np.arange(256 * j, 256 * j + 256), 512 + np.arange(256 * j, 256 * j + 256)])
            ws[l, b] = kmaj(wi[:, cols], 8).reshape(128, BLK)
            b += 1
        for br, nm in ((1, "w_mla_out"), (2, "w_diff_out"), (0, "w_conv_out")):
            ws[l, b] = kmaj(np.asarray(inp[nm][l], np.float32), 4).reshape(128, BLK)
            b += 1
            for half in range(2):
                c0 = 2976 + br * 1024 + half * 512
                ws[l, b] = kmaj(wi[:, c0:c0 + 512], 8).reshape(128, BLK)
                b += 1
        wo = np.asarray(inp["w_out"][l], np.float32)
        for half in range(2):
            ws[l, b] = kmaj(wo[:, half * 512:(half + 1) * 512], 8).reshape(128, BLK)
            b += 1
        wu = np.asarray(inp["w_up"][l], np.float32)
        for k in range(11):
            cols = np.concatenate([np.arange(256 * k, 256 * k + 256), D_FF + np.arange(256 * k, 256 * k + 256)])
            ws[l, b] = kmaj(wu[:, cols], 8).reshape(128, BLK)
            b += 1
        wd = np.asarray(inp["w_down"][l], np.float32)
        for oc in range(8):
            ws[l, b, :, :22 * 128] = kmaj(wd[:, oc * 128:(oc + 1) * 128], 22).reshape(128, 22 * 128)
            b += 1
        assert b == NBLK
    return w1, wuq, wukv, ws


T5_THR = [int(math.ceil(16.0 * 8.0 ** (m / 16.0))) for m in range(1, 16)]


def build(layers, first, final, dbg=False):
    nc = bass.Bass("TRN2", target_bir_lowering=False)
    P = Prog(nc)
    nl = len(layers)

    def din(name, shape, dt):
        return P.dram(name, shape, dt, "ExternalInput").ap()

    def dscr(name, shape, dt):
        return P.dram(name, shape, dt, "ExternalOutput" if dbg else "Internal").ap()

    xin = din("xin", [D, S], F32)
    pos_d = din("pos", [1, S], I32)
    colp_d = din("colp", [128, NCOLP], F32)
    relb_d = din("relb", [1, 128], F32)
    lamv_d = din("lamv", [1, 4 * L * 64], F32)
    w1_d = din("w1", [nl, 128, 8 * 1952], F32)
    wuq_d = din("wuq", [nl, 128, 2 * 768], F32)
    wukv_d = din("wukv", [nl, 128, 1024], F32)
    ws_d = din("ws", [nl * NBLK, 128, BLK], F32)
    yout = P.dram("yout", [D, S], F32, "ExternalOutput").ap()

    xres = dscr("xres", [D, S], F32)
    rope_d = dscr("rope", [2 * 32, S], F32)
    QT_d = dscr("QT", [96, 8 * S], BF16)
    KT_d = dscr("KT", [96, 8 * S], BF16)
    DQ_d = dscr("DQ", [128, 4 * S], BF16)
    DK_d = dscr("DK", [128, 4 * S], BF16)
    oB_d = dscr("oB", [512, S], BF16)
    oC_d = dscr("oC", [512, S], BF16)
    wsb_d = P.dram("wsb", [NBLK, 128, BLK], BF16, "Internal").ap()

    def aps(*xs):
        return [x for x in xs if not isinstance(x, (int, float)) and x is not None]

    def mm(out, lhsT, rhs, start=True, stop=True):
        return P.op("pe", lambda e: e.matmul(out, lhsT=lhsT, rhs=rhs, start=start, stop=stop),
                    reads=[lhsT, rhs], writes=[out])

    def act(out, in_, func, scale=1.0, bias=0.0, eng="act"):
        return P.op(eng, lambda e: e.activation(out=out, in_=in_, func=func, bias=bias, scale=scale),
                    reads=aps(in_, scale, bias), writes=[out])

    def tt(out, a, b, op, eng="dve"):
        return P.op(eng, lambda e: e.tensor_tensor(out=out, in0=a, in1=b, op=op), reads=[a, b], writes=[out])

    def ts(out, a, s1, op0, s2=None, op1=None, eng="dve"):
        if op1 is None:
            return P.op(eng, lambda e: e.tensor_scalar(out=out, in0=a, scalar1=s1, scalar2=None, op0=op0),
                        reads=aps(a, s1), writes=[out])
        return P.op(eng, lambda e: e.tensor_scalar(out=out, in0=a, scalar1=s1, scalar2=s2, op0=op0, op1=op1),
                    reads=aps(a, s1, s2), writes=[out])

    def stt(out, in0, scalar, in1, op0, op1, eng="dve"):
        return P.op(eng, lambda e: e.scalar_tensor_tensor(out=out, in0=in0, scalar=scalar, in1=in1, op0=op0, op1=op1),
                    reads=aps(in0, scalar, in1), writes=[out])

    def cp(out, in_, eng="dve"):
        if eng == "act":
            return act(out, in_, AF.Copy)
        return P.op(eng, lambda e: e.tensor_copy(out=out, in_=in_), reads=[in_], writes=[out])

    def recip(out, in_):
        return P.op("dve", lambda e: e.reciprocal(out=out, in_=in_), reads=[in_], writes=[out])

    def memset(t, v, eng="pool"):
        return P.op(eng, lambda e: e.memset(t, v), writes=[t])

    def dma(out, in_, q="sp"):
        return P.op(q, lambda e: e.dma_start(out=out, in_=in_), reads=[in_], writes=[out], dma=True)

    final_ops = []
    glob = contextlib.ExitStack()
    with glob:
        psb = [P.psum(glob, f"ps{i}", [128, 512], F32) for i in range(8)]
        rot = {"i": 0, "banks": list(range(8))}

        def nps():
            b = rot["banks"][rot["i"] % len(rot["banks"])]
            rot["i"] += 1
            return psb[b]

        colp = P.sbuf(glob, "colp", [128, NCOLP], F32)
        ones_bf = P.sbuf(glob, "ones_bf", [128, 128], BF16)
        ones_f = P.sbuf(glob, "ones_f", [128, 128], F32)
        ident_bf = P.sbuf(glob, "ident_bf", [128, 128], BF16)
        selA = P.sbuf(glob, "selA", [2, 128], F32)
        selB = P.sbuf(glob, "selB", [2, 128], F32)
        osel = P.sbuf(glob, "osel", [128, 4], BF16)
        maskT = P.sbuf(glob, "maskT", [128, 128], F32)
        biasT = P.sbuf(glob, "biasT", [128, 4, 256], F32)
        lam = P.sbuf(glob, "lam", [128, 3 * L], F32)
        epsc = P.sbuf(glob, "epsc", [128, 1], F32)

        dma(colp[:], colp_d)
        memset(ones_bf[:], 1.0)
        memset(ones_f[:], 1.0)
        memset(epsc[:], EPS)
        memset(ident_bf[:], 1.0)
        P.op("pool", lambda e: e.affine_select(out=ident_bf[:], in_=ident_bf[:], compare_op=ALU.is_equal, fill=0.0,
                                               base=0, pattern=[[-1, 128]], channel_multiplier=1),
             reads=[ident_bf[:]], writes=[ident_bf[:]])
        memset(selA[:], 1.0)
        memset(selB[:], 1.0)
        P.op("pool", lambda e: e.affine_select(out=selA[:], in_=selA[:], compare_op=ALU.is_equal, fill=0.0,
                                               base=0, pattern=[[0, 128]], channel_multiplier=1),
             reads=[selA[:]], writes=[selA[:]])
        P.op("pool", lambda e: e.affine_select(out=selB[:], in_=selB[:], compare_op=ALU.is_equal, fill=0.0,
                                               base=-1, pattern=[[0, 128]], channel_multiplier=1),
             reads=[selB[:]], writes=[selB[:]])
        memset(osel[:], 0.0)
        memset(osel[:, 0:1], 1.0)
        memset(osel[:, 3:4], 1.0)

        with contextlib.ExitStack() as st:
            io = P.sbuf(st, "io", [128, 256], I32)
            dd = P.sbuf(st, "dd", [128, 256], F32)
            bk = P.sbuf(st, "bk", [128, 256], F32)
            eq = P.sbuf(st, "eq", [128, 256], F32)
            tb = P.sbuf(st, "tb", [128, 32, 4], F32)
            lv = P.sbuf(st, "lv", [128, 4, L, 64], F32)
            pr = P.sbuf(st, "pr", [128, 2, L, 64], F32)
            e12 = P.sbuf(st, "e12", [128, 2, L], F32)
            P.op("pool", lambda e: e.iota(io[:], pattern=[[1, 256]], base=0, channel_multiplier=-1), writes=[io[:]])
            cp(dd[:], io[:])
            ts(maskT[:], dd[:, 0:128], 0.0, ALU.is_lt, -30000.0, ALU.mult)
            ts(bk[:], dd[:], 0.0, ALU.max, 16.0, ALU.min)
            for thr in T5_THR:
                stt(bk[:], dd[:], float(thr), bk[:], ALU.is_ge, ALU.add)
            dma(tb[:].rearrange("p a b -> p (a b)"), relb_d.partition_broadcast(128))
            tdf = P.sbuf(st, "tdf", [128, 32, 4], F32)
            tt(tdf[:], tb[:], tb[:, 31:32, :].to_broadcast([128, 32, 4]), ALU.subtract)
            for h in range(4):
                ts(biasT[:, h, :], dd[:], 0.0, ALU.is_lt, -30000.0, ALU.mult)
            for b in range(31):
                ts(eq[:], bk[:], float(b), ALU.is_equal)
                for h in range(4):
                    stt(biasT[:, h, :], eq[:], tdf[:, b, h:h + 1], biasT[:, h, :], ALU.mult, ALU.add)
            dma(lv[:].rearrange("p a l d -> p (a l d)"), lamv_d.partition_broadcast(128))
            tt(pr[:, 0], lv[:, 0], lv[:, 1], ALU.mult)
            tt(pr[:, 1], lv[:, 2], lv[:, 3], ALU.mult)
            P.op("dve", lambda e: e.reduce_sum(out=e12[:].rearrange("p a l -> p (a l)"),
                                               in_=pr[:].rearrange("p a l d -> p (a l) d"), axis=AX.X),
                 reads=[pr[:]], writes=[e12[:]])
            act(e12[:], e12[:], AF.Exp)
            tt(lam[:, 0:L], e12[:, 0, :], e12[:, 1, :], ALU.subtract)
            for l in range(L):
                li = 0.8 - 0.6 * math.exp(-0.3 * l)
                ts(lam[:, l:l + 1], lam[:, l:l + 1], float(li), ALU.add)
                ts(lam[:, L + l:L + l + 1], lam[:, l:l + 1], -1.0, ALU.mult)
                c = cpi("diff_sub_norm", l, 0)
                ts(lam[:, 2 * L + l:2 * L + l + 1], colp[:, c:c + 1], float(1.0 - li), ALU.mult)
        with contextlib.ExitStack() as st:
            posi = P.sbuf(st, "posi", [128, S], I32)
            ang = P.sbuf(st, "ang", [128, S], F32)
            t1 = P.sbuf(st, "rt1", [128, S], F32)
            ki = P.sbuf(st, "rki", [128, S], I32)
            R = slice(64, 96)
            fc = cpi("freq", 0, 0)
            dma(posi[R, :], pos_d.partition_broadcast(32))
            cp(ang[R, :], posi[R, :])
            ts(ang[R, :], ang[R, :], colp[R, fc:fc + 1], ALU.mult)
            ts(t1[R, :], ang[R, :], float(1.0 / (2 * math.pi)), ALU.mult)
            cp(ki[R, :], t1[R, :])
            cp(t1[R, :], ki[R, :])
            stt(ang[R, :], t1[R, :], float(-2 * math.pi), ang[R, :], ALU.mult, ALU.add)
            for which, shift in ((1, 0.0), (0, math.pi / 2)):
                ts(t1[R, :], ang[R, :], float(shift), ALU.add)
                for _ in range(2):
                    ts(ki[R, :].bitcast(F32), t1[R, :], float(math.pi), ALU.is_gt, float(-2 * math.pi), ALU.mult)
                    tt(t1[R, :], t1[R, :], ki[R, :].bitcast(F32), ALU.add)
                ts(ki[R, :].bitcast(F32), t1[R, :], float(-math.pi), ALU.is_lt, float(2 * math.pi), ALU.mult)
                tt(t1[R, :], t1[R, :], ki[R, :].bitcast(F32), ALU.add)
                act(t1[R, :], t1[R, :], AF.Sin)
                dma(rope_d[which * 32:(which + 1) * 32, :], t1[R, :])

        xsrc = xin if first else xres
        if not first:
            pass

        def xview(t):
            return t.rearrange("(c p) s -> p c s", p=128)

        def norm_cast(xt, gname, l, hT, sq, rstd_b):
            act(sq[:].rearrange("p c t -> p (c t)"), xt[:].rearrange("p c t -> p (c t)"), AF.Square)
            ps = nps()
            for c in range(8):
                mm(ps[:, :], ones_bf[:, :], sq[:, c, :], start=(c == 0), stop=(c == 7))
            act(rstd_b[:], ps[:, :], AF.Sqrt, scale=1.0 / D, bias=epsc[:, 0:1])
            recip(rstd_b[:], rstd_b[:])
            for c in range(8):
                gi = cpi(gname, l, c)
                stt(hT[:, c, :], xt[:, c, :], colp[:, gi:gi + 1], rstd_b[:], ALU.mult, ALU.mult)

        for li_, l in enumerate(layers):
            lam_init = 0.8 - 0.6 * math.exp(-0.3 * l)
            with contextlib.ExitStack() as stV:
                Vm = P.sbuf(stV, "Vm", [128, 32, 8, 65], BF16)
                Vd = P.sbuf(stV, "Vd", [128, 32, 512], BF16)
                memset(Vm[:].rearrange("p a h d -> p (a h) d")[:, :, 64:65], 1.0)
                with contextlib.ExitStack() as st:
                    w1 = P.sbuf(st, "w1", [128, 8, 1952], BF16)
                    wkr = P.sbuf(st, "wkr", [128, 8, 2, 96], BF16)
                    wuq = P.sbuf(st, "wuq", [128, 2, 8, 96], BF16)
                    wqr = P.sbuf(st, "wqr", [128, 2, 8, 96], BF16)
                    wukv = P.sbuf(st, "wukv", [128, 8, 128], BF16)
                    xt = P.sbuf(st, "xt", [128, 8, TG], F32)
                    hT = P.sbuf(st, "hT", [128, 8, TG], BF16)
                    sq = P.sbuf(st, "sq", [128, 8, TG], BF16)
                    rstd_b = P.sbuf(st, "rstd_b", [128, TG], F32)
                    rcol = P.sbuf(st, "rcol", [128, 4], F32)
                    cqf = P.sbuf(st, "cqf", [128, 2, TG], F32)
                    cqb = P.sbuf(st, "cqb", [128, 2, TG], BF16)
                    sqq = P.sbuf(st, "sqq", [128, 2, TG], BF16)
                    rq_b = P.sbuf(st, "rq_b", [128, TG], F32)
                    ckf = P.sbuf(st, "ckf", [128, TG], F32)
                    ckb = P.sbuf(st, "ckb", [128, TG], BF16)
                    sqk = P.sbuf(st, "sqk", [128, TG], BF16)
                    rk_b = P.sbuf(st, "rk_b", [128, TG], F32)
                    rkcol = P.sbuf(st, "rkcol", [128, 4], F32)
                    cs = P.sbuf(st, "cs", [128, TG], F32)
                    sn = P.sbuf(st, "sn", [128, TG], F32)
                    ta = P.sbuf(st, "ta", [128, TG], F32)
                    tb2 = P.sbuf(st, "tb2", [128, TG], F32)
                    kpe = P.sbuf(st, "kpe", [128, TG], BF16)
                    QTg = P.sbuf(st, "QTg", [96, 8, TG], BF16)
                    KTg = P.sbuf(st, "KTg", [96, 8, TG], BF16)
                    DQg = P.sbuf(st, "DQg", [128, 4, TG], BF16)
                    DKg = P.sbuf(st, "DKg", [128, 4, TG], BF16)

                    dma(w1[:].rearrange("p c n -> p (c n)"), w1_d[li_], q="pool")
                    dma(wuq[:].rearrange("p c h n -> p (c h n)"), wuq_d[li_], q="pool")
                    dma(wukv[:].rearrange("p h n -> p (h n)"), wukv_d[li_], q="pool")
                    memset(wkr[:], 0.0)
                    memset(wqr[:], 0.0)
                    for c in range(8):
                        cp(wkr[:, c, 0, 64:96], w1[:, c, 384:416], eng="pool")
                        ts(wkr[:, c, 1, 64:80], w1[:, c, 400:416], -1.0, ALU.mult, eng="pool")
                        cp(wkr[:, c, 1, 80:96], w1[:, c, 384:400], eng="pool")
                    for c in range(2):
                        ts(wqr[:, c, :, 64:80], wuq[:, c, :, 80:96], -1.0, ALU.mult, eng="pool")
                        cp(wqr[:, c, :, 80:96], wuq[:, c, :, 64:80], eng="pool")

                    for tg in range(NTG):
                        tsl = slice(tg * TG, (tg + 1) * TG)
                        dma(xt[:], xview(xsrc)[:, :, tsl])
                        dma(cs[64:96, :], rope_d[0:32, tsl])
                        dma(sn[64:96, :], rope_d[32:64, tsl])
                        norm_cast(xt, "norm_mix", l, hT, sq, rstd_b)
                        for cc in range(2):
                            ps = nps()
                            for c in range(8):
                                mm(ps[:, :], w1[:, c, cc * 128:(cc + 1) * 128], hT[:, c, :], start=(c == 0), stop=(c == 7))
                            cp(cqf[:, cc, :], ps[:, :], eng=("act" if cc else "dve"))
                        act(sqq[:].rearrange("p c t -> p (c t)"), cqf[:].rearrange("p c t -> p (c t)"), AF.Square)
                        ps = nps()
                        for cc in range(2):
                            mm(ps[:, :], ones_bf[:, :], sqq[:, cc, :], start=(cc == 0), stop=(cc == 1))
                        act(rq_b[:], ps[:, :], AF.Sqrt, scale=1.0 / 256, bias=epsc[:, 0:1])
                        recip(rq_b[:], rq_b[:])
                        for cc in range(2):
                            gi = cpi("mla_q_norm", l, cc)
                            ts(cqb[:, cc, :], cqf[:, cc, :], colp[:, gi:gi + 1], ALU.mult)
                        for h in range(8):
                            pa = nps()
                            pb = nps()
                            for cc in range(2):
                                mm(pa[0:96, :], wuq[:, cc, h, :], cqb[:, cc, :], start=(cc == 0), stop=(cc == 1))
                            for cc in range(2):
                                mm(pb[0:96, :], wqr[:, cc, h, :], cqb[:, cc, :], start=(cc == 0), stop=(cc == 1))
                            tt(QTg[0:64, h, :], pa[0:64, :], rq_b[0:64, :], ALU.mult)
                            tt(ta[64:96, :], pa[64:96, :], cs[64:96, :], ALU.mult)
                            tt(tb2[64:96, :], pb[64:96, :], sn[64:96, :], ALU.mult)
                            tt(ta[64:96, :], ta[64:96, :], tb2[64:96, :], ALU.add)
                            tt(QTg[64:96, h, :], ta[64:96, :], rq_b[64:96, :], ALU.mult)
                        dma(QT_d.rearrange("p (h s) -> p h s", h=8)[:, :, tsl], QTg[:])
                        ps = nps()
                        for c in range(8):
                            mm(ps[:, :], w1[:, c, 256:384], hT[:, c, :], start=(c == 0), stop=(c == 7))
                        cp(ckf[:], ps[:, :], eng="act")
                        act(sqk[:], ckf[:], AF.Square)
                        ps = nps()
                        mm(ps[:, :], ones_bf[:, :], sqk[:], start=True, stop=True)
                        act(rk_b[:], ps[:, :], AF.Sqrt, scale=1.0 / 128, bias=epsc[:, 0:1])
                        recip(rk_b[:], rk_b[:])
                        ps2 = nps()
                        for t in range(4):
                            mm(ps2[:, t:t + 1], sqk[:, t * 128:(t + 1) * 128], ones_bf[:, 0:1], start=True, stop=True)
                        act(rkcol[:], ps2[:, 0:4], AF.Sqrt, scale=1.0 / 128, bias=epsc[:, 0:1])
                        recip(rkcol[:], rkcol[:])
                        gi = cpi("mla_kv_norm", l, 0)
                        ts(ckb[:], ckf[:], colp[:, gi:gi + 1], ALU.mult)
                        for h in range(8):
                            pk = nps()
                            mm(pk[0:64, :], wukv[:, h, 0:64], ckb[:], start=True, stop=True)
                            tt(KTg[0:64, h, :], pk[0:64, :], rk_b[0:64, :], ALU.mult)
                        for t in range(4):
                            pv = nps()
                            mm(pv[:, :].rearrange("p (h d) -> p h d", h=8), ckb[:, t * 128:(t + 1) * 128], wukv[:, :, 64:128],
                               start=True, stop=True)
                            ts(Vm[:, tg * 4 + t, :, 0:64], pv[:, :].rearrange("p (h d) -> p h d", h=8), rkcol[:, t:t + 1], ALU.mult)
                        pa = nps()
                        pb = nps()
                        for c in range(8):
                            mm(pa[0:96, :], wkr[:, c, 0, :], hT[:, c, :], start=(c == 0), stop=(c == 7))
                        for c in range(8):
                            mm(pb[0:96, :], wkr[:, c, 1, :], hT[:, c, :], start=(c == 0), stop=(c == 7))
                        tt(ta[64:96, :], pa[64:96, :], cs[64:96, :], ALU.mult)
                        tt(tb2[64:96, :], pb[64:96, :], sn[64:96, :], ALU.mult)
                        tt(kpe[64:96, :], ta[64:96, :], tb2[64:96, :], ALU.add)
                        cp(KTg[64:96, :, :], kpe[64:96, :].unsqueeze(1).to_broadcast([32, 8, TG]), eng="pool")
                        dma(KT_d.rearrange("p (h s) -> p h s", h=8)[:, :, tsl], KTg[:])
                        for (dst, c0) in ((DQg, 416), (DKg, 928)):
                            for cc in range(4):
                                ps = nps()
                                for c in range(8):
                                    mm(ps[:, :], w1[:, c, c0 + cc * 128:c0 + (cc + 1) * 128], hT[:, c, :], start=(c == 0), stop=(c == 7))
                                cp(dst[:, cc, :], ps[:, :], eng=("act" if cc % 2 else "dve"))
                        dma(DQ_d.rearrange("p (h s) -> p h s", h=4)[:, :, tsl], DQg[:])
                        dma(DK_d.rearrange("p (h s) -> p h s", h=4)[:, :, tsl], DKg[:])
                        for t in range(4):
                            pv = nps()
                            for c in range(8):
                                mm(pv[:, :], hT[:, c, t * 128:(t + 1) * 128], w1[:, c, 1440:1952], start=(c == 0), stop=(c == 7))
                            cp(Vd[:, tg * 4 + t, :], pv[:, :], eng=("act" if t % 2 else "dve"))

                with contextlib.ExitStack() as st:
                    Qh = [P.sbuf(st, f"Qh{i}", [128, S], BF16) for i in range(2)]
                    Kh = [P.sbuf(st, f"Kh{i}", [128, S], BF16) for i in range(2)]
                    pts = [P.sbuf(st, f"pt{i}", [128, TG], BF16) for i in range(8)]
                    tmps = [P.sbuf(st, f"tmp{i}", [128, 256], F32) for i in range(4)]
                    rd = P.sbuf(st, "rd", [128, TG], F32)
                    bcs = [P.sbuf(st, f"bcs{i}", [128, TG], F32) for i in range(2)]
                    t0 = P.sbuf(st, "t0", [128, TG], F32)
                    t1 = P.sbuf(st, "t1", [128, TG], F32)
                    od = P.sbuf(st, "od", [128, TG], F32)
                    sqo = P.sbuf(st, "sqo", [128, TG], BF16)
                    rso = P.sbuf(st, "rso", [128, TG], F32)
                    ohs = [P.sbuf(st, f"oh{i}", [128, TG], BF16) for i in range(2)]
                    cnt = {"pt": 0, "tmp": 0, "oh": 0}
                    rot["banks"] = [3, 4, 5, 6, 7]
                    cb = [P.sbuf(st, f"cb{i}", [128, BLK], BF16) for i in range(4)]
                    pc = {"ld": 0, "st": 0}

                    def precast_some(n, li_=li_):
                        for _ in range(n):
                            while pc["ld"] < min(pc["st"] + 3, NBLK):
                                b_ = pc["ld"]
                                dma(cb[b_ % 4][:], ws_d[li_ * NBLK + b_], q="pool")
                                pc["ld"] += 1
                            if pc["st"] < NBLK:
                                b_ = pc["st"]
                                dma(wsb_d[b_], cb[b_ % 4][:], q="pool")
                                pc["st"] += 1

                    units = [("m", h) for h in range(8)] + [("d", h) for h in range(4)]

                    def load_unit(u, i):
                        kind, h = u
                        if kind == "m":
                            dma(Qh[i][0:96, :], QT_d[:, h * S:(h + 1) * S])
                            dma(Kh[i][0:96, :], KT_d[:, h * S:(h + 1) * S])
                        else:
                            dma(Qh[i][:, :], DQ_d[:, h * S:(h + 1) * S])
                            dma(Kh[i][:, :], DK_d[:, h * S:(h + 1) * S])

                    load_unit(units[0], 0)
                    for ui, u in enumerate(units):
                        kind, h = u
                        if ui + 1 < len(units):
                            load_unit(units[ui + 1], (ui + 1) % 2)
                        Q = Qh[ui % 2]
                        K = Kh[ui % 2]
                        for g in range(NTG):
                            nk = 4 * g + 4
                            if kind == "m":
                                scale = 96 ** -0.5
                                oacc = psb[g % 2]

                                def score(kt, g=g, h=h, scale=scale, Q=Q, K=K):
                                    j = kt - 4 * g
                                    lo = 128 * j if j > 0 else 0
                                    ps = nps()
                                    mm(ps[:, lo:TG], K[0:96, kt * 128:(kt + 1) * 128], Q[0:96, g * TG + lo:(g + 1) * TG])
                                    pt = pts[cnt["pt"] % len(pts)]
                                    cnt["pt"] += 1
                                    if j >= 0:
                                        tm = tmps[cnt["tmp"] % len(tmps)]
                                        cnt["tmp"] += 1
                                        stt(tm[:, 0:128], ps[:, lo:lo + 128], float(scale), maskT[:], ALU.mult, ALU.add)
                                        act(pt[:, lo:lo + 128], tm[:, 0:128], AF.Exp)
                                        if lo + 128 < TG:
                                            act(pt[:, lo + 128:TG], ps[:, lo + 128:TG], AF.Exp, scale=float(scale))
                                    else:
                                        act(pt[:, :], ps[:, :], AF.Exp, scale=float(scale))
                                    return (lo, pt)

                                def pv(kt, st_, g=g, h=h, oacc=oacc, nk=nk):
                                    lo, pt = st_
                                    mm(oacc[0:65, lo:TG], Vm[:, kt, h, :], pt[:, lo:TG], start=(kt == 0), stop=(kt == nk - 1))

                                pend = []
                                for kt in range(nk):
                                    pend.append((kt, score(kt)))
                                    if len(pend) > 2:
                                        pv(*pend.pop(0))
                                while pend:
                                    pv(*pend.pop(0))
                                recip(rd[64:65, :], oacc[64:65, :])
                                bp = nps()
                                mm(bp[0:64, :], ones_f[64:65, 0:64], rd[64:65, :])
                                bc = bcs[g % 2]
                                act(bc[0:64, :], bp[0:64, :], AF.Copy)
                                oh = ohs[cnt["oh"] % 2]
                                cnt["oh"] += 1
                                tt(oh[0:64, :], oacc[0:64, :], bc[0:64, :], ALU.mult)
                                dma(oB_d[h * 64:(h + 1) * 64, g * TG:(g + 1) * TG], oh[0:64, :])
                            else:
                                scale = 64 ** -0.5
                                oa = (psb[0], psb[1])
                                den = psb[2]

                                def score(kt, g=g, h=h, scale=scale, Q=Q, K=K):
                                    j = kt - 4 * g
                                    lo = 128 * j if j > 0 else 0
                                    if j >= 0:
                                        b0, b1, bo = lo, min(lo + 256, TG), 0
                                    elif j == -1:
                                        b0, b1, bo = 0, 128, 128
                                    else:
                                        b0 = b1 = bo = 0
                                    res_ = []
                                    for m in range(2):
                                        ps = nps()
                                        mm(ps[:, lo:TG], K[64 * m:64 * m + 64, kt * 128:(kt + 1) * 128],
                                           Q[64 * m:64 * m + 64, g * TG + lo:(g + 1) * TG])
                                        pt = pts[cnt["pt"] % len(pts)]
                                        cnt["pt"] += 1
                                        if b1 > b0:
                                            tm = tmps[cnt["tmp"] % len(tmps)]
                                            cnt["tmp"] += 1
                                            w = b1 - b0
                                            stt(tm[:, 0:w], ps[:, b0:b1], float(scale), biasT[:, h, bo:bo + w], ALU.mult, ALU.add)
                                            act(pt[:, b0:b1], tm[:, 0:w], AF.Exp)
                                            if b1 < TG:
                                                act(pt[:, b1:TG], ps[:, b1:TG], AF.Exp, scale=float(scale))
                                        else:
                                            act(pt[:, :], ps[:, :], AF.Exp, scale=float(scale))
                                        res_.append(pt)
                                    return (lo, res_)

                                def pv(kt, st_, g=g, h=h, oa=oa, den=den, nk=nk):
                                    lo, ptl = st_
                                    for m in range(2):
                                        pt = ptl[m]
                                        mm(oa[m][:, lo:TG], Vd[:, kt, h * 128:(h + 1) * 128], pt[:, lo:TG],
                                           start=(kt == 0), stop=(kt == nk - 1))
                                        mm(den[0:2, lo:TG], osel[:, 2 * m:2 * m + 2], pt[:, lo:TG],
                                           start=(kt == 0 and m == 0), stop=(kt == nk - 1 and m == 1))

                                pend = []
                                for kt in range(nk):
                                    pend.append((kt, score(kt)))
                                    if len(pend) > 1:
                                        pv(*pend.pop(0))
                                while pend:
                                    pv(*pend.pop(0))
                                recip(rd[0:2, :], den[0:2, :])
                                for m, sel in ((0, selA), (1, selB)):
                                    bp = nps()
                                    mm(bp[:, :], sel[0:2, :], rd[0:2, :])
                                    act(bcs[m][:, :], bp[:, :], AF.Copy)
                                tt(t0[:], oa[0][:, :], bcs[0][:, :], ALU.mult)
                                tt(t1[:], oa[1][:, :], bcs[1][:, :], ALU.mult)
                                stt(od[:], t1[:], lam[:, L + l:L + l + 1], t0[:], ALU.mult, ALU.add)
                                act(sqo[:], od[:], AF.Square)
                                sp_ = nps()
                                mm(sp_[:, :], ones_bf[:, :], sqo[:])
                                act(rso[:], sp_[:, :], AF.Sqrt, scale=1.0 / 128, bias=epsc[:, 0:1])
                                recip(rso[:], rso[:])
                                tt(od[:], od[:], rso[:], ALU.mult)
                                oh = ohs[cnt["oh"] % 2]
                                cnt["oh"] += 1
                                act(oh[:, :], od[:], AF.Identity, scale=lam[:, 2 * L + l:2 * L + l + 1])
                                dma(oC_d[h * 128:(h + 1) * 128, g * TG:(g + 1) * TG], oh[:, :])
                        precast_some(3 if ui < 8 else 2)
                    precast_some(NBLK)
                    rot["banks"] = list(range(8))

            with contextlib.ExitStack() as st:
                NR = 5
                ring = [P.sbuf(st, f"wr{i}", [128, BLK], BF16) for i in range(NR)]
                diag = P.sbuf(st, "diag", [128, 124, 128], BF16)
                xt = P.sbuf(st, "xt3", [128, 8, TG], F32)
                hT = P.sbuf(st, "hT3", [128, 8, TG], BF16)
                sq = P.sbuf(st, "sq3", [128, 8, TG], BF16)
                rstd_b = P.sbuf(st, "rstd3", [128, TG], F32)
                aT = P.sbuf(st, "aT", [128, 4, 30 + TG], BF16)
                wk = [P.sbuf(st, f"wk{i}", [128, TG], F32) for i in range(4)]
                halo = P.sbuf(st, "halo", [128, 44, 2], F32)
                cnt3 = {"wk": 0, "sb": 0, "ub": 0, "blk": 0}

                def nwk():
                    cnt3["wk"] += 1
                    return wk[cnt3["wk"] % 4]

                for j in range(31):
                    for c in range(4):
                        ci = cpi("conv_w", l, j * 4 + c)
                        ts(diag[:, j * 4 + c, :], ident_bf[:], colp[:, ci:ci + 1], ALU.mult, eng="pool")
                memset(aT[:, :, 0:30], 0.0)
                memset(halo[:], 0.0)

                stream = {"next": 0}
                total_blocks = NTG * NBLK

                def prefetch():
                    n = stream["next"]
                    if n < total_blocks:
                        dma(ring[n % NR][:], wsb_d[n % NBLK])
                        stream["next"] = n + 1

                def getblk(k):
                    while stream["next"] <= min(k + 2, total_blocks - 1):
                        prefetch()
                    return ring[k % NR]

                kblk = 0
                for tg in range(NTG):
                    tsl = slice(tg * TG, (tg + 1) * TG)
                    dma(xt[:], xview(xsrc)[:, :, tsl])
                    stA = contextlib.ExitStack()
                    af = P.sbuf(stA, "af", [128, 4, TG], F32)
                    afb = P.sbuf(stA, "afb", [128, 4, TG], BF16)
                    sqa = P.sbuf(stA, "sqa", [128, 4, TG], BF16)
                    mu = P.sbuf(stA, "mu", [128, TG], F32)
                    rl = P.sbuf(stA, "rl", [128, TG], F32)
                    sT = P.sbuf(stA, "sT", [128, 4, TG], BF16)
                    oBt = P.sbuf(stA, "oBt", [128, 4, TG], BF16)
                    oCt = P.sbuf(stA, "oCt", [128, 4, TG], BF16)
                    macc = P.sbuf(stA, "macc", [128, 8, TG], F32)
                    mT = P.sbuf(stA, "mT", [128, 8, TG], BF16)
                    dma(oBt[:], oB_d.rearrange("(c p) s -> p c s", p=128)[:, :, tsl])
                    dma(oCt[:], oC_d.rearrange("(c p) s -> p c s", p=128)[:, :, tsl])
                    norm_cast(xt, "norm_mix", l, hT, sq, rstd_b)
                    for jb in range(2):
                        wb = getblk(kblk).rearrange("p (c n) -> p c n", c=8)
                        kblk += 1
                        for i in range(2):
                            pv = nps()
                            pg = nps()
                            for c in range(8):
                                mm(pv[:, :], wb[:, c, i * 128:(i + 1) * 128], hT[:, c, :], start=(c == 0), stop=(c == 7))
                            for c in range(8):
                                mm(pg[:, :], wb[:, c, 256 + i * 128:256 + (i + 1) * 128], hT[:, c, :], start=(c == 0), stop=(c == 7))
                            w0 = nwk()
                            act(w0[:], pg[:, :], AF.Sigmoid)
                            tt(aT[:, jb * 2 + i, 30:30 + TG], pv[:, :], w0[:], ALU.mult)
                    for c in range(4):
                        pc = nps()
                        for j in range(31):
                            mm(pc[:, :], diag[:, j * 4 + c, :], aT[:, c, j:j + TG], start=(j == 0), stop=(j == 30))
                        bi = cpi("conv_b", l, c)
                        act(af[:, c, :], pc[:, :], AF.Identity, bias=colp[:, bi:bi + 1])
                        act(sqa[:, c, :], pc[:, :], AF.Square, bias=colp[:, bi:bi + 1])
                        cp(afb[:, c, :], af[:, c, :], eng="pool")
                        cp(aT[:, c, 0:30], aT[:, c, TG:TG + 30], eng="pool")
                    def branch(br, src, pos_, kblk):
                        wo_ = getblk(kblk).rearrange("p (c n) -> p c n", c=4)
                        kblk += 1
                        for half in range(2):
                            wg = getblk(kblk).rearrange("p (c n) -> p c n", c=8)
                            kblk += 1
                            for o4 in range(4):
                                oc = half * 4 + o4
                                py = nps()
                                pg = nps()
                                for c in range(8):
                                    mm(pg[:, :], wg[:, c, o4 * 128:(o4 + 1) * 128], hT[:, c, :], start=(c == 0), stop=(c == 7))
                                for c in range(4):
                                    mm(py[:, :], wo_[:, c, oc * 128:(oc + 1) * 128], src[:, c, :], start=(c == 0), stop=(c == 3))
                                w0 = nwk()
                                bi = cpi("gate_bias", l, br * 8 + oc)
                                act(w0[:], pg[:, :], AF.Sigmoid, bias=colp[:, bi:bi + 1])
                                if pos_ == 0:
                                    tt(macc[:, oc, :], w0[:], py[:, :], ALU.mult)
                                else:
                                    tt(w0[:], w0[:], py[:, :], ALU.mult)
                                    if pos_ == 1:
                                        tt(macc[:, oc, :], macc[:, oc, :], w0[:], ALU.add, eng="pool")
                                    else:
                                        tt(mT[:, oc, :], macc[:, oc, :], w0[:], ALU.add, eng="pool")
                        return kblk

                    kblk = branch(1, oBt, 0, kblk)
                    p1 = nps()
                    p2 = nps()
                    for c in range(4):
                        mm(p1[:, :], ones_bf[:, :], afb[:, c, :], start=(c == 0), stop=(c == 3))
                    for c in range(4):
                        mm(p2[:, :], ones_bf[:, :], sqa[:, c, :], start=(c == 0), stop=(c == 3))
                    ts(mu[:], p1[:, :], 1.0 / 512, ALU.mult)
                    w0 = nwk()
                    tt(w0[:], mu[:], mu[:], ALU.mult)
                    stt(rl[:], p2[:, :], 1.0 / 512, w0[:], ALU.mult, ALU.subtract)
                    act(rl[:], rl[:], AF.Sqrt, bias=epsc[:, 0:1])
                    recip(rl[:], rl[:])
                    for c in range(4):
                        w0 = nwk()
                        tt(w0[:], af[:, c, :], mu[:], ALU.subtract)
                        tt(w0[:], w0[:], rl[:], ALU.mult)
                        gi = cpi("conv_ln_g", l, c)
                        bi = cpi("conv_ln_b", l, c)
                        act(sT[:, c, :], w0[:], AF.Silu, scale=colp[:, gi:gi + 1], bias=colp[:, bi:bi + 1])
                    kblk = branch(2, oCt, 1, kblk)
                    kblk = branch(0, sT, 2, kblk)
                    for half in range(2):
                        wo_ = getblk(kblk).rearrange("p (c n) -> p c n", c=8)
                        kblk += 1
                        for o4 in range(4):
                            oc = half * 4 + o4
                            po = nps()
                            for c in range(8):
                                mm(po[:, :], wo_[:, c, o4 * 128:(o4 + 1) * 128], mT[:, c, :], start=(c == 0), stop=(c == 7))
                            tt(xt[:, oc, :], xt[:, oc, :], po[:, :], ALU.add)
                    stA.close()
                    stB = contextlib.ExitStack()
                    sb_ = [P.sbuf(stB, f"sbf{i}", [128, 2 + TG], F32) for i in range(4)]
                    ub = [P.sbuf(stB, f"ub{i}", [128, TG], F32) for i in range(4)]
                    actT = P.sbuf(stB, "actT", [128, 22, TG], BF16)
                    norm_cast(xt, "norm_ffn", l, hT, sq, rstd_b)
                    for kb in range(11):
                        wu = getblk(kblk).rearrange("p (c n) -> p c n", c=8)
                        kblk += 1
                        for i in range(2):
                            pi = kb * 2 + i
                            us = []
                            for part in range(2):
                                ch = pi + 22 * part
                                pu = nps()
                                for c in range(8):
                                    mm(pu[:, :], wu[:, c, part * 256 + i * 128:part * 256 + (i + 1) * 128], hT[:, c, :],
                                       start=(c == 0), stop=(c == 7))
                                cnt3["sb"] += 1
                                s_ = sb_[cnt3["sb"] % 4]
                                cnt3["ub"] += 1
                                u_ = ub[cnt3["ub"] % 4]
                                cp(s_[:, 0:2], halo[:, ch, :], eng="pool")
                                cp(s_[:, 2:2 + TG], pu[:, :], eng="act")
                                cp(halo[:, ch, :], s_[:, TG:TG + 2], eng="pool")
                                k2 = cpi("ffn_conv_w", l, 2 * 44 + ch)
                                k1 = cpi("ffn_conv_w", l, 1 * 44 + ch)
                                k0 = cpi("ffn_conv_w", l, 0 * 44 + ch)
                                bi = cpi("ffn_conv_b", l, ch)
                                act(u_[:], s_[:, 2:2 + TG], AF.Identity, scale=colp[:, k2:k2 + 1], bias=colp[:, bi:bi + 1])
                                stt(u_[:], s_[:, 1:1 + TG], colp[:, k1:k1 + 1], u_[:], ALU.mult, ALU.add)
                                stt(u_[:], s_[:, 0:TG], colp[:, k0:k0 + 1], u_[:], ALU.mult, ALU.add)
                                us.append(u_)
                            act(us[0][:], us[0][:], AF.Silu)
                            tt(actT[:, pi, :], us[0][:], us[1][:], ALU.mult)
                    for oc in range(8):
                        wd = getblk(kblk).rearrange("p (c n) -> p c n", c=32)
                        kblk += 1
                        po = nps()
                        for c in range(22):
                            mm(po[:, :], wd[:, c, :], actT[:, c, :], start=(c == 0), stop=(c == 21))
                        tt(xt[:, oc, :], xt[:, oc, :], po[:, :], ALU.add)
                    dma(xview(xres)[:, :, tsl], xt[:])
                    stB.close()
                assert kblk == total_blocks
            xsrc = xres

        if final:
            with contextlib.ExitStack() as st:
                xt = P.sbuf(st, "xtf", [128, 8, TG], F32)
                sq = P.sbuf(st, "sqf", [128, 8, TG], BF16)
                rstd_b = P.sbuf(st, "rstdf", [128, TG], F32)
                yt = [P.sbuf(st, f"ytf{i}", [128, 8, TG], F32) for i in range(2)]
                for tg in range(NTG):
                    tsl = slice(tg * TG, (tg + 1) * TG)
                    dma(xt[:], xview(xsrc)[:, :, tsl])
                    act(sq[:].rearrange("p c t -> p (c t)"), xt[:].rearrange("p c t -> p (c t)"), AF.Square)
                    ps = nps()
                    for c in range(8):
                        mm(ps[:, :], ones_bf[:, :], sq[:, c, :], start=(c == 0), stop=(c == 7))
                    act(rstd_b[:], ps[:, :], AF.Sqrt, scale=1.0 / D, bias=epsc[:, 0:1])
                    recip(rstd_b[:], rstd_b[:])
                    y = yt[tg % 2]
                    for c in range(8):
                        gi = COLP_OFF["norm_final"][0] + c
                        stt(y[:, c, :], xt[:, c, :], colp[:, gi:gi + 1], rstd_b[:], ALU.mult, ALU.mult)
                    final_ops.append(dma(xview(yout)[:, :, tsl], y[:]))
        else:
            with contextlib.ExitStack() as st:
                xt = P.sbuf(st, "xtc", [128, 8, TG], F32)
                for tg in range(NTG):
                    tsl = slice(tg * TG, (tg + 1) * TG)
                    dma(xt[:], xview(xsrc)[:, :, tsl])
                    final_ops.append(dma(xview(yout)[:, :, tsl], xt[:]))
        P.emit(final_ops)
    return nc, P


LAYER_GROUPS = [[0, 1, 2, 3]]


def kernel(**inp):
    x = np.asarray(inp["x"], np.float32)
    B = x.shape[0]
    colp = host_colp(inp)
    w1, wuq, wukv, ws = host_weights(inp)
    relb = np.ascontiguousarray(np.asarray(inp["rel_bias"], np.float32).reshape(1, 128))
    lamv = np.ascontiguousarray(np.stack([np.asarray(inp[k], np.float32) for k in
                                          ("diff_lam_q1", "diff_lam_k1", "diff_lam_q2", "diff_lam_k2")]).reshape(1, -1))
    pos = np.asarray(inp["positions"], np.int32)
    cur = [np.ascontiguousarray(x[b].T) for b in range(B)]
    for gi, grp in enumerate(LAYER_GROUPS):
        nc, _ = build(grp, first=True, final=(gi == len(LAYER_GROUPS) - 1))
        in_maps = []
        for b in range(B):
            in_maps.append({
                "xin": cur[b], "pos": np.ascontiguousarray(pos[b][None, :]), "colp": colp, "relb": relb, "lamv": lamv,
                "w1": np.ascontiguousarray(w1[grp].reshape(len(grp), 128, -1)),
                "wuq": np.ascontiguousarray(wuq[grp].reshape(len(grp), 128, -1)),
                "wukv": np.ascontiguousarray(wukv[grp]),
                "ws": np.ascontiguousarray(ws[grp].reshape(len(grp) * NBLK, 128, BLK)),
            })
        res = run_bass_kernel_spmd(nc, in_maps, core_ids=list(range(B)))
        cur = [np.asarray(res.results[b]["yout"]) for b in range(B)]
    out = np.stack([c.T for c in cur]).astype(np.float32)
    return out
```
